# Optimizing a Trainium2 kernel written in Bass

```python
import math
import jax, jax.numpy as jnp
from jax import lax
import numpy as np

D_MODEL = 1024
BATCH = 32
SEQ = 2048
DEPTH = 2

EPS = 1e-6
N_BRANCH = 3
BRANCH_W = D_MODEL // 2
SSM_GROUP = 16
SSM_GROUPS = BRANCH_W // SSM_GROUP
SSM_STATE = 64
DT_MIN = 1e-3
DT_MAX = 1e-1
HEAD_DIM = 64
ATT_HEADS = BRANCH_W // HEAD_DIM
DILATED_PATTERNS = ((128, 1), (512, 4), (2048, 16))
N_PAT = len(DILATED_PATTERNS)
ATT_BLOCK = 128
ATT_SCALE = HEAD_DIM ** -0.5
CONV_WIDTH = 31
D_FF = -(-(8 * D_MODEL) // (3 * 256)) * 256

COL_U = BRANCH_W
COL_Q = N_PAT * BRANCH_W
COL_KV = BRANCH_W
COL_CONV = 2 * BRANCH_W
COL_GATE = N_BRANCH * D_MODEL
SPLIT_POINTS = (COL_U,
                COL_U + COL_Q,
                COL_U + COL_Q + COL_KV,
                COL_U + COL_Q + 2 * COL_KV,
                COL_U + COL_Q + 2 * COL_KV + COL_CONV)
IN_COLS = SPLIT_POINTS[-1] + COL_GATE

kernel_name = 'hybrid_s5_dilated_attn_conformer_conv_block'


def rms_norm(x, g):
    xf = x.astype(jnp.float32)
    y = xf * lax.rsqrt(jnp.mean(xf * xf, axis=-1, keepdims=True) + EPS)
    return (y * g.astype(jnp.float32)).astype(x.dtype)


def _complex_affine_combine(e1, e2):
    a1r, a1i, b1r, b1i = e1
    a2r, a2i, b2r, b2i = e2
    return (a1r * a2r - a1i * a2i,
            a1r * a2i + a1i * a2r,
            a2r * b1r - a2i * b1i + b2r,
            a2r * b1i + a2i * b1r + b2i)


def s5_branch(u, lam_re, lam_im, log_dt, b_re, b_im, c_re, c_im, d_skip, w_glu):
    bsz, seq, _ = u.shape
    f32 = jnp.float32
    uf = u.astype(f32).reshape(bsz, seq, SSM_GROUPS, SSM_GROUP)
    lam_re = lam_re.astype(f32)
    lam_im = lam_im.astype(f32)
    dt = jnp.exp(log_dt.astype(f32))[:, None]
    mag = jnp.exp(lam_re * dt)
    ab_re = mag * jnp.cos(lam_im * dt)
    ab_im = mag * jnp.sin(lam_im * dt)
    nr, ni = ab_re - 1.0, ab_im
    den = lam_re * lam_re + lam_im * lam_im
    z_re = ((nr * lam_re + ni * lam_im) / den)[..., None]
    z_im = ((ni * lam_re - nr * lam_im) / den)[..., None]
    b_re = b_re.astype(f32)
    b_im = b_im.astype(f32)
    bb_re = z_re * b_re - z_im * b_im
    bb_im = z_re * b_im + z_im * b_re
    bu_re = jnp.einsum('blgh,gph->blgp', uf, bb_re)
    bu_im = jnp.einsum('blgh,gph->blgp', uf, bb_im)
    a_re = jnp.broadcast_to(ab_re, (1, seq, SSM_GROUPS, SSM_STATE))
    a_im = jnp.broadcast_to(ab_im, (1, seq, SSM_GROUPS, SSM_STATE))
    _, _, s_re, s_im = lax.associative_scan(
        _complex_affine_combine, (a_re, a_im, bu_re, bu_im), axis=1)
    y = (jnp.einsum('blgp,ghp->blgh', s_re, c_re.astype(f32))
         - jnp.einsum('blgp,ghp->blgh', s_im, c_im.astype(f32)))
    y = y.reshape(bsz, seq, BRANCH_W) + d_skip.astype(f32) * uf.reshape(bsz, seq, BRANCH_W)
    y = jax.nn.gelu(y).astype(u.dtype)
    z = y @ w_glu
    return z[..., :D_MODEL] * jax.nn.sigmoid(z[..., D_MODEL:])


def _dilated_group(q, k, v, window, dilation):
    bsz, seq, nh, hd = q.shape
    ls = seq // dilation
    nb = -(-ls // ATT_BLOCK)
    lp = nb * ATT_BLOCK
    w_sub = window // dilation

    def to_sub(t):
        t = t.reshape(bsz, ls, dilation, nh, hd).transpose(0, 2, 3, 1, 4)
        t = jnp.pad(t, ((0, 0), (0, 0), (0, 0), (0, lp - ls), (0, 0)))
        return t.reshape(bsz, dilation, nh, nb, ATT_BLOCK, hd)

    def with_prev(t):
        prev = jnp.pad(t, ((0, 0), (0, 0), (0, 0), (1, 0), (0, 0), (0, 0)))[:, :, :, :-1]
        return jnp.concatenate([prev, t], axis=4)

    qb = to_sub(q)
    kc = with_prev(to_sub(k))
    vc = with_prev(to_sub(v))
    s = jnp.einsum('bdhnqe,bdhnke->bdhnqk', qb, kc).astype(jnp.float32) * ATT_SCALE
    qi = jnp.arange(ATT_BLOCK)[:, None]
    kj = jnp.arange(2 * ATT_BLOCK)[None, :]
    dist = qi - kj + ATT_BLOCK
    kpos = jnp.arange(nb)[:, None, None] * ATT_BLOCK + kj[None] - ATT_BLOCK
    valid = (dist >= 0) & (dist <= w_sub) & (kpos >= 0)
    s = jnp.where(valid, s, -jnp.inf)
    m = jnp.max(s, axis=-1, keepdims=True)
    p = jnp.exp(s - m)
    den = jnp.sum(p, axis=-1, keepdims=True)
    o = jnp.einsum('bdhnqk,bdhnke->bdhnqe', p, vc.astype(jnp.float32)) / den
    lse = (m + jnp.log(den))[..., 0]
    o = o.reshape(bsz, dilation, nh, lp, hd)[:, :, :, :ls]
    o = o.transpose(0, 3, 1, 2, 4).reshape(bsz, seq, nh, hd)
    lse = lse.reshape(bsz, dilation, nh, lp)[:, :, :, :ls]
    lse = lse.transpose(0, 3, 1, 2).reshape(bsz, seq, nh)
    return o, lse


def dilated_attention(q, k, v):
    outs, lses = [], []
    for p_idx, (window, dilation) in enumerate(DILATED_PATTERNS):
        o, lse = _dilated_group(q[:, :, p_idx], k, v, window, dilation)
        outs.append(o)
        lses.append(lse)
    wts = jax.nn.softmax(jnp.stack(lses, axis=0), axis=0)
    return jnp.sum(wts[..., None] * jnp.stack(outs, axis=0), axis=0)


def conformer_conv(cv, conv_w, conv_b, ln_g, ln_b, w_pw2):
    a, g = jnp.split(cv, 2, axis=-1)
    h = a * jax.nn.sigmoid(g)
    h = lax.conv_general_dilated(
        h, conv_w[:, None, :], window_strides=(1,),
        padding=[(CONV_WIDTH - 1, 0)],
        dimension_numbers=('NWC', 'WIO', 'NWC'),
        feature_group_count=BRANCH_W) + conv_b
    hf = h.astype(jnp.float32)
    mu = jnp.mean(hf, axis=-1, keepdims=True)
    var = jnp.mean(jnp.square(hf - mu), axis=-1, keepdims=True)
    hn = (hf - mu) * lax.rsqrt(var + EPS) * ln_g.astype(jnp.float32) + ln_b.astype(jnp.float32)
    h = jax.nn.silu(hn).astype(cv.dtype)
    return h @ w_pw2


def setup_inputs(seed: int = 0) -> dict:
    key = jax.random.key(seed)
    ks = jax.random.split(key, 24)
    f32 = jnp.float32

    def nrm(k, shape, scale):
        return jax.random.normal(k, shape, f32) * scale

    lam_im_base = jnp.pi * jnp.arange(SSM_STATE, dtype=f32)
    return {
        'x': nrm(ks[0], (BATCH, SEQ, D_MODEL), 1.0),
        'norm1_g': 1.0 + nrm(ks[1], (DEPTH, D_MODEL), 0.05),
        'w_in': nrm(ks[2], (DEPTH, D_MODEL, IN_COLS), D_MODEL ** -0.5),
        'b_gate': nrm(ks[3], (DEPTH, COL_GATE), 0.02),
        'ssm_lambda_re': -0.5 + nrm(ks[4], (DEPTH, SSM_GROUPS, SSM_STATE), 0.01),
        'ssm_lambda_im': lam_im_base + nrm(ks[5], (DEPTH, SSM_GROUPS, SSM_STATE), 0.01),
        'ssm_log_dt': jax.random.uniform(ks[6], (DEPTH, SSM_GROUPS), f32,
                                         math.log(DT_MIN), math.log(DT_MAX)),
        'ssm_b_re': nrm(ks[7], (DEPTH, SSM_GROUPS, SSM_STATE, SSM_GROUP), (2 * SSM_GROUP) ** -0.5),
        'ssm_b_im': nrm(ks[8], (DEPTH, SSM_GROUPS, SSM_STATE, SSM_GROUP), (2 * SSM_GROUP) ** -0.5),
        'ssm_c_re': nrm(ks[9], (DEPTH, SSM_GROUPS, SSM_GROUP, SSM_STATE), SSM_STATE ** -0.25),
        'ssm_c_im': nrm(ks[10], (DEPTH, SSM_GROUPS, SSM_GROUP, SSM_STATE), SSM_STATE ** -0.25),
        'ssm_d': nrm(ks[11], (DEPTH, BRANCH_W), 1.0),
        'w_ssm_glu': nrm(ks[12], (DEPTH, BRANCH_W, 2 * D_MODEL), BRANCH_W ** -0.5),
        'w_att_up': nrm(ks[13], (DEPTH, BRANCH_W, D_MODEL), BRANCH_W ** -0.5),
        'conv_w': nrm(ks[14], (DEPTH, CONV_WIDTH, BRANCH_W), CONV_WIDTH ** -0.5),
        'conv_b': nrm(ks[15], (DEPTH, BRANCH_W), 0.02),
        'conv_ln_g': 1.0 + nrm(ks[16], (DEPTH, BRANCH_W), 0.05),
        'conv_ln_b': nrm(ks[17], (DEPTH, BRANCH_W), 0.02),
        'w_conv_pw2': nrm(ks[18], (DEPTH, BRANCH_W, D_MODEL), BRANCH_W ** -0.5),
        'w_out': nrm(ks[19], (DEPTH, D_MODEL, D_MODEL), D_MODEL ** -0.5),
        'norm2_g': 1.0 + nrm(ks[20], (DEPTH, D_MODEL), 0.05),
        'w_ffn_in': nrm(ks[21], (DEPTH, D_MODEL, 2 * D_FF), D_MODEL ** -0.5),
        'w_ffn_out': nrm(ks[22], (DEPTH, D_FF, D_MODEL), D_FF ** -0.5),
        'final_g': 1.0 + nrm(ks[23], (D_MODEL,), 0.05),
    }


def reference(x, norm1_g, w_in, b_gate, ssm_lambda_re, ssm_lambda_im, ssm_log_dt,
              ssm_b_re, ssm_b_im, ssm_c_re, ssm_c_im, ssm_d, w_ssm_glu, w_att_up,
              conv_w, conv_b, conv_ln_g, conv_ln_b, w_conv_pw2, w_out,
              norm2_g, w_ffn_in, w_ffn_out, final_g):
    bsz, seq, _ = x.shape
    for l in range(DEPTH):
        h = rms_norm(x, norm1_g[l])
        proj = h @ w_in[l]
        u, q, k, v, cv, g = jnp.split(proj, SPLIT_POINTS, axis=-1)
        y_s = s5_branch(u, ssm_lambda_re[l], ssm_lambda_im[l], ssm_log_dt[l],
                        ssm_b_re[l], ssm_b_im[l], ssm_c_re[l], ssm_c_im[l],
                        ssm_d[l], w_ssm_glu[l])
        q = q.reshape(bsz, seq, N_PAT, ATT_HEADS, HEAD_DIM)
        k = k.reshape(bsz, seq, ATT_HEADS, HEAD_DIM)
        v = v.reshape(bsz, seq, ATT_HEADS, HEAD_DIM)
        o = dilated_attention(q, k, v).astype(x.dtype).reshape(bsz, seq, BRANCH_W)
        y_a = o @ w_att_up[l]
        y_c = conformer_conv(cv, conv_w[l], conv_b[l], conv_ln_g[l], conv_ln_b[l], w_conv_pw2[l])
        gate = jax.nn.sigmoid((g + b_gate[l]).astype(jnp.float32)).astype(x.dtype)
        gate = gate.reshape(bsz, seq, N_BRANCH, D_MODEL)
        merged = gate[:, :, 0] * y_s + gate[:, :, 1] * y_a + gate[:, :, 2] * y_c
        x = x + merged @ w_out[l]
        h = rms_norm(x, norm2_g[l])
        z = h @ w_ffn_in[l]
        x = x + (jax.nn.silu(z[..., :D_FF]) * z[..., D_FF:]) @ w_ffn_out[l]
    return rms_norm(x, final_g)
```

```python
import math
import os
from contextlib import ExitStack
import numpy as np
import concourse.bass as bass
import concourse.mybir as mybir
from concourse.bass_utils import run_bass_kernel_spmd

F32 = mybir.dt.float32
BF16 = mybir.dt.bfloat16
AF = mybir.ActivationFunctionType
ALU = mybir.AluOpType
AX = mybir.AxisListType

L = 2048
D = 1024
DFF = 2816
NEG = -30000.0
NCONST = 128 * 5 + 8 + 1
PAD = 1024


class Buf:
    __slots__ = ("lw", "rd", "ds", "dso", "excl")

    def __init__(self, excl=False):
        self.excl = excl
        self.lw = None
        self.rd = {}
        self.ds = None
        self.dso = None


class Sched:
    ENG = ("pe", "act", "dve", "pool", "sp")

    def __init__(self, nc):
        self.nc = nc
        self.ops = {e: [] for e in self.ENG}
        self.cnt = {e: 0 for e in self.ENG}
        self.waited = {e: {} for e in self.ENG}
        self.dcnt = {}
        self.nobar = set()

    def _deps(self, reads, writes):
        deps = {}
        for b in reads:
            if b.lw is not None and deps.get(b.lw[0], 0) < b.lw[1]:
                deps[b.lw[0]] = b.lw[1]
            if b.excl:
                for k, v in b.rd.items():
                    if deps.get(k, 0) < v:
                        deps[k] = v
        for b in writes:
            if b.lw is not None and deps.get(b.lw[0], 0) < b.lw[1]:
                deps[b.lw[0]] = b.lw[1]
            for k, v in b.rd.items():
                if deps.get(k, 0) < v:
                    deps[k] = v
        return deps

    def _waits(self, eng, deps, skip_self):
        w = []
        wd = self.waited[eng]
        for k, v in deps.items():
            if skip_self and k == eng:
                continue
            if wd.get(k, 0) >= v:
                continue
            wd[k] = v
            w.append((k, v))
        return w

    def _fin(self, ev, reads, writes):
        k, v = ev
        for b in reads:
            if b.rd.get(k, 0) < v:
                b.rd[k] = v
        for b in writes:
            b.lw = ev
            b.rd = {}

    def op(self, eng, fn, reads=(), writes=()):
        w = self._waits(eng, self._deps(reads, writes), eng == "pe")
        self.cnt[eng] += 1
        ev = (eng, self.cnt[eng])
        self.ops[eng].append((w, fn, (eng, 1)))
        self._fin(ev, reads, writes)

    def dma(self, q, dsem, fn, reads=(), writes=()):
        w = self._waits(q, self._deps(reads, writes), False)
        key = ("d", dsem)
        self.dcnt[key] = self.dcnt.get(key, 0) + 16
        ev = (key, self.dcnt[key])
        self.ops[q].append((w, fn, (key, 16)))
        self._fin(ev, reads, writes)

    def barrier(self, full=False):
        for e in self.ENG:
            deps = {k: v for k, v in self.cnt.items() if v > 0}
            deps.update({k: v for k, v in self.dcnt.items() if full or k not in self.nobar})
            w = self._waits(e, deps, True)
            if w:
                self.ops[e].append((w, None, None))

    def run(self, stack):
        nc = self.nc
        sems = {}
        for e in self.ENG:
            sems[e] = stack.enter_context(nc.semaphore("s_" + e))
        for key in self.dcnt:
            sems[key] = stack.enter_context(nc.semaphore("d%d" % key[1]))
        block = stack.enter_context(nc.Block())
        ops = self.ops

        def replay(e, eng):
            for w, fn, inc in ops[e]:
                for k, v in w:
                    eng.wait_ge(sems[k], v)
                if fn is not None:
                    fn(eng).then_inc(sems[inc[0]], inc[1])

        @block.tensor
        def _(eng):
            replay("pe", eng)

        @block.scalar
        def _(eng):
            replay("act", eng)

        @block.vector
        def _(eng):
            replay("dve", eng)

        @block.gpsimd
        def _(eng):
            replay("pool", eng)

        @block.sync
        def _(eng):
            replay("sp", eng)


class KB:
    def __init__(self, nc):
        self.nc = nc
        self.S = Sched(nc)
        self.nds = 0

    def sb(self, st, name, shape, dt):
        self.nsb = getattr(self, "nsb", 0) + 1
        if not hasattr(self, "bufs"):
            self.bufs = {}
        b = self.bufs.get(name)
        if b is None:
            b = self.bufs[name] = Buf()
        return st.enter_context(self.nc.sbuf_tensor("%s_u%d" % (name, self.nsb), shape, dt)), b

    def load(self, dst, dbuf, src, sbuf=None, q="sp", slow=False):
        if dbuf.ds is None:
            dbuf.ds = self.nds
            self.nds += 1
        kw = {"allow_slow_non_contiguous": True} if slow else {}
        self.S.dma(q, dbuf.ds, lambda e: e.dma_start(out=dst, in_=src, **kw),
                   reads=[sbuf] if sbuf is not None else [], writes=[dbuf])

    def store(self, dst, dbuf, src, sbuf, q="sp"):
        if sbuf.dso is None:
            sbuf.dso = self.nds
            self.nds += 1
        self.S.dma(q, sbuf.dso, lambda e: e.dma_start(out=dst, in_=src), reads=[sbuf], writes=[dbuf])

    def mm(self, out, lhsT, rhs, start, stop, reads, writes, tp=None):
        kw = {} if tp is None else {"tile_position": tp}
        self.S.op("pe", lambda e: e.matmul(out, lhsT=lhsT, rhs=rhs, start=start, stop=stop, **kw), reads, writes)

    def tr(self, out, in_, ident, reads, writes):
        self.S.op("pe", lambda e: e.transpose(out=out, in_=in_, identity=ident), reads, writes)

    def act(self, out, in_, func, reads, writes, bias=None, scale=None):
        kw = {}
        if bias is not None:
            kw["bias"] = bias
        if scale is not None:
            kw["scale"] = scale
        self.S.op("act", lambda e: e.activation(out=out, in_=in_, func=func, **kw), reads, writes)

    def tt(self, eng, out, in0, in1, op, reads, writes):
        self.S.op(eng, lambda e: e.tensor_tensor(out=out, in0=in0, in1=in1, op=op), reads, writes)

    def ts(self, eng, out, in0, s1, s2, op0, op1, reads, writes):
        if s2 is None:
            self.S.op(eng, lambda e: e.tensor_scalar(out=out, in0=in0, scalar1=s1, scalar2=None, op0=op0), reads, writes)
        else:
            self.S.op(eng, lambda e: e.tensor_scalar(out=out, in0=in0, scalar1=s1, scalar2=s2, op0=op0, op1=op1),
                      reads, writes)

    def stt(self, eng, out, in0, scalar, in1, op0, op1, reads, writes):
        self.S.op(eng, lambda e: e.scalar_tensor_tensor(out=out, in0=in0, scalar=scalar, in1=in1, op0=op0, op1=op1),
                  reads, writes)

    def copy(self, eng, out, in_, reads, writes):
        if eng == "act":
            self.act(out, in_, AF.Copy, reads, writes)
        else:
            self.S.op(eng, lambda e: e.tensor_copy(out=out, in_=in_), reads, writes)

    def memset(self, eng, ap, val, writes):
        self.S.op(eng, lambda e: e.memset(ap, val), (), writes)

    def recip(self, out, in_, reads, writes):
        self.S.op("dve", lambda e: e.reciprocal(out=out, in_=in_), reads, writes)

    def rsum(self, out, in_, reads, writes):
        self.S.op("dve", lambda e: e.reduce_sum(out=out, in_=in_, axis=AX.X), reads, writes)


PNAMES = [
    ("norm1_g", (D,)), ("w_in", (D, 7168)), ("b_gate", (3072,)), ("ssm_lambda_re", (32, 64)),
    ("ssm_lambda_im", (32, 64)), ("ssm_log_dt", (32,)), ("ssm_b_re", (32, 64, 16)), ("ssm_b_im", (32, 64, 16)),
    ("ssm_c_re", (32, 16, 64)), ("ssm_c_im", (32, 16, 64)), ("ssm_d", (512,)), ("w_ssm_glu", (512, 2048)),
    ("w_att_up", (512, D)), ("conv_w", (31, 512)), ("conv_b", (512,)), ("conv_ln_g", (512,)),
    ("conv_ln_b", (512,)), ("w_conv_pw2", (512, D)), ("w_out", (D, D)), ("norm2_g", (D,)),
    ("w_ffn_in", (D, 2 * DFF)), ("w_ffn_out", (DFF, D)),
]
BIGW = ["w_in", "w_ssm_glu", "w_att_up", "w_conv_pw2", "w_out", "w_ffn_in", "w_ffn_out"]
PATTERNS = ((128, 1), (512, 4), (2048, 16))
PSTOP = int(os.environ.get('PSTOP', '9'))
B1STOP = int(os.environ.get('B1STOP', '9'))
B1VAR = os.environ.get('B1VAR', '')
S5ID = int(os.environ.get('S5ID', '0'))
FUSEA = int(os.environ.get('FUSEA', '0'))
AMASK = os.environ.get('AMASK', 'pool')
PH = set(os.environ.get('KPH', 'prep,A,B1,B2,B3,C,E').split(','))


def build(NSEQ, DEPTH):
    nc = bass.Bass("TRN2", target_bir_lowering=False)
    kb = KB(nc)
    S = kb.S
    x_d = nc.dram_tensor("x", [NSEQ * L, D], F32, kind="ExternalInput").ap()
    out_d = nc.dram_tensor("out", [NSEQ * L, D], F32, kind="ExternalOutput").ap()
    consts_d = nc.dram_tensor("consts", [128, NCONST], F32, kind="ExternalInput").ap()
    final_g_d = nc.dram_tensor("final_g", [1, D], F32, kind="ExternalInput").ap()
    P = {}
    for n, shp in PNAMES:
        P[n] = nc.dram_tensor(n, [DEPTH] + list(shp), F32, kind="ExternalInput").ap()
    WB = {}
    WBbuf = {}
    for n in BIGW:
        shp = dict(PNAMES)[n]
        WB[n] = nc.dram_tensor(n + "_bf", [DEPTH] + list(shp), BF16, kind="Internal").ap()
        WBbuf[n] = [Buf() for _ in range(DEPTH)]
    qpow_d = nc.dram_tensor("qpow", [DEPTH, 32, 128, 16 * 128], BF16, kind="Internal").ap()
    qpow_b = [Buf() for _ in range(DEPTH)]
    bst_d = nc.dram_tensor("bst_d", [DEPTH, 128, 4096], BF16, kind="Internal").ap()
    cst_d = nc.dram_tensor("cst_d", [DEPTH, 128, 4096], BF16, kind="Internal").ap()
    bc_b = [Buf() for _ in range(DEPTH)]
    xmid_d = nc.dram_tensor("xmid", [L, D], F32, kind="Internal").ap()
    xout_d = nc.dram_tensor("xout", [L, D], F32, kind="Internal").ap()
    xmid_b = [Buf() for _ in range(16)]
    xout_b = [Buf() for _ in range(16)]
    out_b = Buf()

    with ExitStack() as st0:
        for l in range(DEPTH):
            for n in BIGW:
                K = dict(PNAMES)[n][0]
                for r in range(0, K, 128):
                    kb.load(WB[n][l, r:r + 128, :], WBbuf[n][l], P[n][l, r:r + 128, :], q="pool")
                S.nobar.add(("d", WBbuf[n][l].ds))

        cst, cst_b = kb.sb(st0, "cst", [128, NCONST], F32)
        kb.load(cst[:], cst_b, consts_d)
        ident_f = cst[:, 0:128]
        jswap_f = cst[:, 128:256]
        ones_f = cst[:, 512:640]
        rowmask = cst[:, 640:648]
        sgn = cst[:, 648:649]
        cbf, cbf_b = kb.sb(st0, "cbf", [128, 640], BF16)
        kb.copy("dve", cbf[:], cst[:, 0:640], [cst_b], [cbf_b])
        ident_b = cbf[:, 0:128]
        mcur_b = cbf[:, 256:384]
        mprev_b = cbf[:, 384:512]
        ones_b = cbf[:, 512:640]
        m01, m01_b = kb.sb(st0, "m01", [128, 256], BF16)
        kb.ts("dve", m01[:], cst[:, 256:512], 0.0, None, ALU.is_equal, None, [cst_b], [m01_b])

        psb = []
        for i in range(8):
            t = st0.enter_context(nc.psum_tensor("ps%d" % i, [128, 512], F32))
            psb.append((t, Buf(excl=True)))
        psi = [0]

        def nextps(pool=(0, 1, 2, 3, 4, 5, 6, 7)):
            psi[0] += 1
            return psb[pool[psi[0] % len(pool)]]

        NSLOT = 4
        ring = [kb.sb(st0, "wr%d" % i, [128, 4096], BF16) for i in range(NSLOT)]
        ri = [0]

        def loadw(name, l, kc, c0, ncols):
            t, b = ring[ri[0] % NSLOT]
            ri[0] += 1
            dst = t[:, 0:kc * ncols].rearrange("p (k n) -> p k n", k=kc)
            src = WB[name][l, :, c0:c0 + ncols].rearrange("(k p) n -> p k n", p=128)
            kb.load(dst, b, src, WBbuf[name][l])
            return dst, b

        s5c = []
        for l in (range(DEPTH) if 'prep' in PH else ()):
            dsk, dsk_b = kb.sb(st0, "dsk%d" % l, [128, 4], F32)
            s5c.append((dsk, dsk_b))
            with ExitStack() as st:
                Bst, Bst_b = kb.sb(st, "Bst%d" % l, [128, 32, 128], BF16)
                Cst, Cst_b = kb.sb(st, "Cst%d" % l, [128, 32, 128], BF16)
                def T(name, w, dt=F32):
                    return kb.sb(st, "%s_%d" % (name, l), [128, w], dt)
                lr, lr_b = T("lr", 32)
                li, li_b = T("li", 32)
                ldt, ldt_b = T("ldt", 32)
                for h in (0, 64):
                    kb.load(lr[h:h + 64, :], lr_b, P["ssm_lambda_re"][l].rearrange("g p -> p g"), slow=True)
                    kb.load(li[h:h + 64, :], li_b, P["ssm_lambda_im"][l].rearrange("g p -> p g"), slow=True)
                kb.load(ldt[:], ldt_b, P["ssm_log_dt"][l:l + 1, :].broadcast_to([128, 32]))
                X1, X1_b = T("X1", 512)
                X2, X2_b = T("X2", 512)
                bre = P["ssm_b_re"][l].rearrange("g p h -> p g h")
                bim = P["ssm_b_im"][l].rearrange("g p h -> p g h")
                X1v = X1[:].rearrange("p (g h) -> p g h", h=16)
                X2v = X2[:].rearrange("p (g h) -> p g h", h=16)
                kb.load(X1v[0:64], X1_b, bre)
                kb.load(X1v[64:128], X1_b, bim)
                kb.load(X2v[0:64], X2_b, bim)
                kb.load(X2v[64:128], X2_b, bre)
                CT, CT_b = T("CT", 512)
                for c in range(4):
                    kb.load(CT[:, c * 128:c * 128 + 64], CT_b,
                            P["ssm_c_re"][l, 8 * c:8 * c + 8].rearrange("g h p -> (g h) p"))
                    kb.load(CT[:, c * 128 + 64:c * 128 + 128], CT_b,
                            P["ssm_c_im"][l, 8 * c:8 * c + 8].rearrange("g h p -> (g h) p"))
                kb.load(dsk[:], dsk_b, P["ssm_d"][l].rearrange("(c p) -> p c", p=128), slow=True)
                dt_, dt_b = T("dt", 32)
                kb.act(dt_[:], ldt[:], AF.Exp, [ldt_b], [dt_b])
                t1, t1_b = T("t1", 32)
                t2, t2_b = T("t2", 32)
                t3, t3_b = T("t3", 32)
                mag, mag_b = T("mag", 32)
                cc, cc_b = T("cc", 32)
                ss, ss_b = T("ss", 32)
                kb.tt("dve", t1[:], lr[:], dt_[:], ALU.mult, [lr_b, dt_b], [t1_b])
                kb.act(mag[:], t1[:], AF.Exp, [t1_b], [mag_b])
                kb.tt("dve", t2[:], li[:], dt_[:], ALU.mult, [li_b, dt_b], [t2_b])
                kb.ts("dve", t2[:], t2[:], 1.0 / 16.0, None, ALU.mult, None, [t2_b], [t2_b])
                kb.act(ss[:], t2[:], AF.Sin, [t2_b], [ss_b])
                kb.ts("dve", t3[:], t2[:], math.pi / 2, None, ALU.add, None, [t2_b], [t3_b])
                kb.act(cc[:], t3[:], AF.Sin, [t3_b], [cc_b])
                for _ in range(4):
                    kb.tt("dve", t1[:], cc[:], cc[:], ALU.mult, [cc_b], [t1_b])
                    kb.tt("dve", t2[:], ss[:], ss[:], ALU.mult, [ss_b], [t2_b])
                    kb.tt("dve", t3[:], cc[:], ss[:], ALU.mult, [cc_b, ss_b], [t3_b])
                    kb.tt("dve", cc[:], t1[:], t2[:], ALU.subtract, [t1_b, t2_b], [cc_b])
                    kb.ts("dve", ss[:], t3[:], 2.0, None, ALU.mult, None, [t3_b], [ss_b])
                ar, ar_b = T("ar", 32)
                ai, ai_b = T("ai", 32)
                kb.tt("dve", ar[:], mag[:], cc[:], ALU.mult, [mag_b, cc_b], [ar_b])
                kb.tt("dve", ai[:], mag[:], ss[:], ALU.mult, [mag_b, ss_b], [ai_b])
                nr, nr_b = T("nr", 32)
                kb.ts("dve", nr[:], ar[:], -1.0, None, ALU.add, None, [ar_b], [nr_b])
                den, den_b = T("den", 32)
                kb.tt("dve", t1[:], lr[:], lr[:], ALU.mult, [lr_b], [t1_b])
                kb.tt("dve", t2[:], li[:], li[:], ALU.mult, [li_b], [t2_b])
                kb.tt("dve", den[:], t1[:], t2[:], ALU.add, [t1_b, t2_b], [den_b])
                kb.recip(den[:], den[:], [den_b], [den_b])
                zr, zr_b = T("zr", 32)
                zi, zi_b = T("zi", 32)
                kb.tt("dve", t1[:], nr[:], lr[:], ALU.mult, [nr_b, lr_b], [t1_b])
                kb.tt("dve", t2[:], ai[:], li[:], ALU.mult, [ai_b, li_b], [t2_b])
                kb.tt("dve", t3[:], t1[:], t2[:], ALU.add, [t1_b, t2_b], [t3_b])
                kb.tt("dve", zr[:], t3[:], den[:], ALU.mult, [t3_b, den_b], [zr_b])
                kb.tt("dve", t1[:], ai[:], lr[:], ALU.mult, [ai_b, lr_b], [t1_b])
                kb.tt("dve", t2[:], nr[:], li[:], ALU.mult, [nr_b, li_b], [t2_b])
                kb.tt("dve", t3[:], t1[:], t2[:], ALU.subtract, [t1_b, t2_b], [t3_b])
                kb.tt("dve", zi[:], t3[:], den[:], ALU.mult, [t3_b, den_b], [zi_b])
                kb.ts("dve", zi[:], zi[:], sgn, None, ALU.mult, None, [zi_b, cst_b], [zi_b])
                zrb = zr[:].unsqueeze(2).broadcast_to([128, 32, 16])
                zib = zi[:].unsqueeze(2).broadcast_to([128, 32, 16])
                kb.tt("dve", X1v, X1v, zrb, ALU.mult, [X1_b, zr_b], [X1_b])
                kb.tt("dve", X2v, X2v, zib, ALU.mult, [X2_b, zi_b], [X2_b])
                kb.tt("dve", X1[:], X1[:], X2[:], ALU.add, [X1_b, X2_b], [X1_b])
                kb.memset("dve", Cst[:], 0.0, [Cst_b])
                for c in (range(4) if PSTOP >= 2 else ()):
                    pt, pb = nextps()
                    kb.tr(pt[:, 0:128], X1[:, c * 128:(c + 1) * 128], ident_f, [X1_b, cst_b], [pb])
                    for gi in range(8):
                        kb.ts("dve", Bst[:, 8 * c + gi, :], pt[:, 0:128], rowmask[:, gi:gi + 1], None, ALU.mult, None,
                              [pb, cst_b], [Bst_b])
                    kb.ts("dve", CT[:, c * 128 + 64:c * 128 + 128], CT[:, c * 128 + 64:c * 128 + 128], -1.0, None,
                          ALU.mult, None, [CT_b], [CT_b])
                    pt2, pb2 = nextps()
                    kb.tr(pt2[:, 0:128], CT[:, c * 128:(c + 1) * 128], ident_f, [CT_b, cst_b], [pb2])
                    for gi in range(8):
                        kb.copy("act", Cst[:, 8 * c + gi, gi * 16:(gi + 1) * 16], pt2[:, gi * 16:(gi + 1) * 16],
                                [pb2], [Cst_b])
                NQ = 16
                arK, arK_b = T("arK", NQ * 32)
                aiK, aiK_b = T("aiK", NQ * 32)
                c2K, c2K_b = T("c2K", NQ * 32)

                def qs(q):
                    return slice(q * 32, (q + 1) * 32)

                def cmul(qo, qa_, qb_):
                    xr, xi = arK[:, qs(qa_)], aiK[:, qs(qa_)]
                    yr, yi = arK[:, qs(qb_)], aiK[:, qs(qb_)]
                    kb.tt("dve", t1[:], xr, yr, ALU.mult, [arK_b], [t1_b])
                    kb.tt("dve", t2[:], xi, yi, ALU.mult, [aiK_b], [t2_b])
                    kb.tt("dve", arK[:, qs(qo)], t1[:], t2[:], ALU.subtract, [t1_b, t2_b], [arK_b])
                    kb.tt("dve", t1[:], xr, yi, ALU.mult, [arK_b, aiK_b], [t1_b])
                    kb.tt("dve", t2[:], xi, yr, ALU.mult, [arK_b, aiK_b], [t2_b])
                    kb.tt("dve", aiK[:, qs(qo)], t1[:], t2[:], ALU.add, [t1_b, t2_b], [aiK_b])
                kb.copy("dve", arK[:, 0:32], ar[:], [ar_b], [arK_b])
                kb.copy("dve", aiK[:, 0:32], ai[:], [ai_b], [aiK_b])
                for k in range(5):
                    cmul(3 * k + 1, 3 * k, 3 * k)
                    cmul(3 * k + 2, 3 * k + 1, 3 * k)
                    cmul(3 * k + 3, 3 * k + 1, 3 * k + 1)
                kb.ts("dve", c2K[:], aiK[:], sgn, -1.0, ALU.mult, ALU.mult, [aiK_b, cst_b], [c2K_b])
                stg = [kb.sb(st, "stg%d_%d" % (i, l), [128, NQ * 128], BF16) for i in range(2)]
                stmp = [kb.sb(st, "stmp%d_%d" % (i, l), [128, NQ * 128], BF16) for i in range(2)]
                idb = ident_f.unsqueeze(1).broadcast_to([128, NQ, 128])
                jsb = jswap_f.unsqueeze(1).broadcast_to([128, NQ, 128])
                for g in (range(32) if PSTOP >= 3 else ()):
                    sg, sg_b = stg[g % 2]
                    tm, tm_b = stmp[g % 2]
                    sgv = sg[:].rearrange("p (k n) -> p k n", k=NQ)
                    tmv = tm[:].rearrange("p (k n) -> p k n", k=NQ)
                    arv = arK[:].rearrange("p (k g) -> p k g", g=32)[:, :, g:g + 1].broadcast_to([128, NQ, 128])
                    c2v = c2K[:].rearrange("p (k g) -> p k g", g=32)[:, :, g:g + 1].broadcast_to([128, NQ, 128])
                    kb.tt("dve", tmv, jsb, c2v, ALU.mult, [cst_b, c2K_b], [tm_b])
                    kb.tt("dve", sgv, idb, arv, ALU.mult, [cst_b, arK_b], [sg_b])
                    kb.tt("dve", sg[:], sg[:], tm[:], ALU.add, [sg_b, tm_b], [sg_b])
                    kb.store(qpow_d[l, g], qpow_b[l], sg[:], sg_b)
                if PSTOP >= 4:
                    kb.store(bst_d[l], bc_b[l], Bst[:].rearrange("p g n -> p (g n)"), Bst_b)
                    kb.store(cst_d[l], bc_b[l], Cst[:].rearrange("p g n -> p (g n)"), Cst_b)
            S.barrier()

        if FUSEA:
            hT, hT_b = kb.sb(st0, "hT", [128, 8, L], BF16)
        for sq in range(NSEQ):
            for l in range(DEPTH):
                last = (l == DEPTH - 1)
                dsk, dsk_b = s5c[l] if s5c else (None, None)

                def xsrc(i):
                    if l == 0:
                        return x_d[sq * L + i * 128: sq * L + (i + 1) * 128, :], None
                    return xout_d[i * 128:(i + 1) * 128, :], xout_b[i]

                with ExitStack() as stB:
                    if not FUSEA:
                        hT, hT_b = kb.sb(stB, "hT", [128, 8, L], BF16)
                    ysT, ysT_b = kb.sb(stB, "ysT", [128, 4, L], BF16)
                    oT, oT_b = kb.sb(stB, "oT", [128, 4, L], BF16)
                    hcT, hcT_b = kb.sb(stB, "hcT", [128, 4, L], BF16)
                    if 'A' in PH and (l == 0 or not FUSEA):
                        with ExitStack() as st:
                            gbc, gbc_b = kb.sb(st, "gbc", [128, D], F32)
                            kb.load(gbc[:], gbc_b, P["norm1_g"][l:l + 1, :].broadcast_to([128, D]))
                            xt = [kb.sb(st, "xt%d" % i, [128, D], F32) for i in range(2)]
                            sqt, sqt_b = kb.sb(st, "sqt", [128, D], F32)
                            hb = [kb.sb(st, "hb%d" % i, [128, D], BF16) for i in range(2)]
                            sm = [kb.sb(st, "sm%d" % i, [128, 4], F32) for i in range(2)]
                            for i in range(16):
                                xa, xb_ = xt[i % 2]
                                src, srcb = xsrc(i)
                                kb.load(xa[:], xb_, src, srcb)
                                kb.act(sqt[:], xa[:], AF.Square, [xb_], [sqt_b])
                                s_, s_b = sm[i % 2]
                                kb.rsum(s_[:, 0:1], sqt[:], [sqt_b], [s_b])
                                kb.ts("dve", s_[:, 1:2], s_[:, 0:1], 1.0 / D, 1e-6, ALU.mult, ALU.add, [s_b], [s_b])
                                kb.act(s_[:, 2:3], s_[:, 1:2], AF.Sqrt, [s_b], [s_b])
                                kb.recip(s_[:, 3:4], s_[:, 2:3], [s_b], [s_b])
                                ha, hab = hb[i % 2]
                                kb.stt("dve", ha[:], xa[:], s_[:, 3:4], gbc[:], ALU.mult, ALU.mult, [xb_, s_b, gbc_b], [hab])
                                pt, pb = nextps()
                                ptb = pt[:].bitcast(BF16)
                                for c in range(8):
                                    kb.tr(ptb[:, c * 128:(c + 1) * 128], ha[:, c * 128:(c + 1) * 128], ident_b, [hab, cbf_b], [pb])
                                kb.copy("act", hT[:, :, i * 128:(i + 1) * 128],
                                        ptb[:, 0:1024].rearrange("p (c t) -> p c t", c=8), [pb], [hT_b])
                    S.barrier()

                    if 'B1' in PH:
                        with ExitStack() as st:
                            uT, uT_b = kb.sb(st, "uT", [128, L], BF16)
                            Bst, Bst_b = kb.sb(st, "BstL", [128, 32, 128], BF16)
                            Cst, Cst_b = kb.sb(st, "CstL", [128, 32, 128], BF16)
                            kb.load(Bst[:].rearrange("p g n -> p (g n)"), Bst_b, bst_d[l], bc_b[l])
                            kb.load(Cst[:].rearrange("p g n -> p (g n)"), Cst_b, cst_d[l], bc_b[l])
                            NLANE = int(os.environ.get('NLANE', '2'))
                            sbb = [[kb.sb(st, "sbb%d_%d" % (ln, i), [128, PAD + L], BF16) for i in range(2)] for ln in range(NLANE)]
                            qp = [[kb.sb(st, "qp%d_%d" % (ln, i), [128, 16, 128], BF16) for i in range(2)] for ln in range(NLANE)]
                            ytmp, ytmp_b = kb.sb(st, "ytmp", [128, 512], F32)
                            for ln in range(NLANE):
                                for i in range(2):
                                    kb.memset("pool", sbb[ln][i][0][:, 0:PAD], 0.0, [sbb[ln][i][1]])
                            evi = [0]

                            def evac(dst_ap, dstb, pt, pb):
                                evi[0] += 1
                                kb.copy("act" if evi[0] % 2 else "dve", dst_ap, pt[:], [pb], [dstb])
                            for c in range(4):
                                wv, wb_ = loadw("w_in", l, 8, c * 128, 128)
                                for b in range(4):
                                    pt, pb = nextps((0, 1, 2, 3))
                                    for kc in range(8):
                                        kb.mm(pt[:], wv[:, kc, :], hT[:, kc, b * 512:(b + 1) * 512], kc == 0, kc == 7,
                                              [wb_, hT_b], [pb])
                                    kb.copy("act", uT[:, b * 512:(b + 1) * 512], pt[:], [pb], [uT_b])
                                for g0 in range(0, 8, NLANE):
                                    lanes = []
                                    for ln in range(min(NLANE, 8 - g0)):
                                        gi = g0 + ln
                                        g = 8 * c + gi
                                        qa, qab = qp[ln][(g0 // NLANE) % 2]
                                        kb.load(qa[:], qab, qpow_d[l, g].rearrange("p (k n) -> p k n", k=16), qpow_b[l])
                                        lanes.append((gi, g, qa, qab, sbb[ln]))
                                    for gi, g, qa, qab, sb_ in lanes:
                                        for b in range(4):
                                            pt, pb = nextps((0, 1, 2, 3))
                                            kb.mm(pt[:], Bst[:, g, :], uT[:, b * 512:(b + 1) * 512], True, True, [Bst_b, uT_b], [pb])
                                            evac(sb_[0][0][:, PAD + b * 512:PAD + (b + 1) * 512], sb_[0][1], pt, pb)
                                    for k in range(6):
                                        base = 4 ** k
                                        for gi, g, qa, qab, sb_ in lanes:
                                            src, srcb = sb_[k % 2]
                                            dst, dstb = sb_[(k + 1) % 2]
                                            for b in range(4):
                                                mats = []
                                                for j in ((1, 2, 3) if k < 5 else (1,)):
                                                    m = j * base
                                                    if (b + 1) * 512 - m > 0:
                                                        mats.append((qa[:, 3 * k + j - 1, :], qab, PAD + b * 512 - m))
                                                dsl = slice(PAD + b * 512, PAD + (b + 1) * 512)
                                                if not mats:
                                                    kb.copy("pool", dst[:, dsl], src[:, dsl], [srcb], [dstb])
                                                    continue
                                                if S5ID:
                                                    mats = [(ident_b, cbf_b, PAD + b * 512)] + mats
                                                pt, pb = nextps((0, 1, 2, 3))
                                                for mi, (lh, lhb, o0) in enumerate(mats):
                                                    kb.mm(pt[:], lh, src[:, o0:o0 + 512], mi == 0, mi == len(mats) - 1,
                                                          [lhb, srcb], [pb])
                                                if S5ID:
                                                    evac(dst[:, dsl], dstb, pt, pb)
                                                else:
                                                    kb.tt("dve", dst[:, dsl], pt[:], src[:, dsl], ALU.add, [pb, srcb], [dstb])
                                    for gi, g, qa, qab, sb_ in lanes:
                                        fin, finb = sb_[0]
                                        for b in range(4):
                                            yt, yb = psb[4 + b]
                                            kb.mm(yt[:], Cst[:, g, :], fin[:, PAD + b * 512:PAD + (b + 1) * 512], gi == 0, gi == 7,
                                                  [Cst_b, finb], [yb])
                                for b in range(4):
                                    yt, yb = psb[4 + b]
                                    kb.stt("dve", ytmp[:], uT[:, b * 512:(b + 1) * 512], dsk[:, c:c + 1], yt[:], ALU.mult, ALU.add,
                                           [uT_b, dsk_b, yb], [ytmp_b])
                                    kb.act(ysT[:, c, b * 512:(b + 1) * 512], ytmp[:], AF.Gelu_apprx_tanh, [ytmp_b], [ysT_b])
                    S.barrier()

                    if 'B2' in PH:
                        with ExitStack() as st:
                            qT = [kb.sb(st, "qT%d" % p, [128, L], BF16) for p in range(3)]
                            kAB = [kb.sb(st, "kAB%d" % i, [128, L], BF16) for i in range(2)]
                            Vp = [kb.sb(st, "Vp%d" % p, [128, 16, 128], BF16) for p in range(3)]
                            acc, acc_b = kb.sb(st, "acc", [128, 2, L], F32)
                            NPT = 6
                            pT = [kb.sb(st, "pT%d" % i, [128, 256], BF16) for i in range(NPT)]
                            rden, rden_b = kb.sb(st, "rden", [128, L], F32)
                            vT, vT_b = kb.sb(st, "vT", [128, L], BF16)
                            kb.memset("pool", kAB[0][0][64:128, :], 0.0, [kAB[0][1]])
                            kb.memset("pool", kAB[1][0][0:64, :], 0.0, [kAB[1][1]])
                            pti = 0
                            for hp in range(4):
                                for p in range(3):
                                    wv, wb_ = loadw("w_in", l, 8, 512 + p * 512 + hp * 128, 128)
                                    for b in range(4):
                                        pt, pb = nextps()
                                        for kc in range(8):
                                            kb.mm(pt[:], wv[:, kc, :], hT[:, kc, b * 512:(b + 1) * 512], kc == 0, kc == 7,
                                                  [wb_, hT_b], [pb])
                                        kb.act(qT[p][0][:, b * 512:(b + 1) * 512], pt[:], AF.Copy, [pb], [qT[p][1]], scale=0.125)
                                wv, wb_ = loadw("w_in", l, 8, 2048 + hp * 128, 128)
                                for b in range(4):
                                    pt, pb = nextps()
                                    for kc in range(8):
                                        kb.mm(pt[:], wv[:, kc, :], hT[:, kc, b * 512:(b + 1) * 512], kc == 0, kc == 7,
                                              [wb_, hT_b], [pb])
                                    kb.copy("act", kAB[0][0][0:64, b * 512:(b + 1) * 512], pt[0:64, :], [pb], [kAB[0][1]])
                                    kb.copy("dve", kAB[1][0][64:128, b * 512:(b + 1) * 512], pt[64:128, :], [pb], [kAB[1][1]])
                                wv, wb_ = loadw("w_in", l, 8, 2560 + hp * 128, 128)
                                for b in range(4):
                                    pt, pb = nextps()
                                    for kc in range(8):
                                        kb.mm(pt[:], wv[:, kc, :], hT[:, kc, b * 512:(b + 1) * 512], kc == 0, kc == 7,
                                              [wb_, hT_b], [pb])
                                    kb.copy("dve", vT[:, b * 512:(b + 1) * 512], pt[:], [pb], [vT_b])
                                for p, (win, d) in enumerate(PATTERNS):
                                    nblk = L // d // 128
                                    blks = [(r, n) for r in range(d) for n in range(nblk)]
                                    for i0 in range(0, 16, 8):
                                        pt, pb = nextps()
                                        ptb = pt[:].bitcast(BF16)
                                        for j in range(8):
                                            r, n = blks[i0 + j]
                                            t0 = r + d * 128 * n
                                            kb.tr(ptb[:, j * 128:(j + 1) * 128], vT[:, t0:t0 + 127 * d + 1:d], ident_b, [vT_b, cbf_b], [pb])
                                        kb.copy("act", Vp[p][0][:, i0:i0 + 8, :],
                                                ptb[:, 0:1024].rearrange("p (j e) -> p j e", j=8), [pb], [Vp[p][1]])
                                pend = []

                                def flush(keep):
                                    while len(pend) > keep:
                                        pend.pop(0)()
                                for p, (win, d) in enumerate(PATTERNS):
                                    nblk = L // d // 128
                                    qt_, qtb = qT[p]
                                    vt_, vtb = Vp[p]
                                    for r in range(d):
                                        for n in range(nblk):
                                            t0 = r + d * 128 * n
                                            tq = slice(t0, t0 + 127 * d + 1, d)
                                            tprev = slice(t0 - 128 * d, t0 - d + 1, d)
                                            hasp = n >= 1
                                            wq = 256 if hasp else 128
                                            pnh = {}
                                            for hh in range(2):
                                                kt_, ktb = kAB[hh]
                                                ps_, psb_ = nextps()
                                                pm = AMASK == 'pool'
                                                kb.mm(ps_[:, 0:128], kt_[:, tq], qt_[:, tq], True, pm, [ktb, qtb], [psb_])
                                                if not pm:
                                                    kb.mm(ps_[:, 0:128], ident_b, mcur_b, False, True, [cbf_b], [psb_])
                                                if hasp:
                                                    kb.mm(ps_[:, 128:256], kt_[:, tprev], qt_[:, tq], True, pm, [ktb, qtb], [psb_])
                                                    if not pm:
                                                        kb.mm(ps_[:, 128:256], ident_b, mprev_b, False, True, [cbf_b], [psb_])
                                                pe_, peb = pT[pti % NPT]
                                                pti += 1
                                                kb.act(pe_[:, 0:wq], ps_[:, 0:wq], AF.Exp, [psb_], [peb])
                                                if pm:
                                                    kb.tt("pool", pe_[:, 0:wq], pe_[:, 0:wq], m01[:, 0:wq], ALU.mult, [peb, m01_b], [peb])

                                                def pv(hh=hh, pe_=pe_, peb=peb, pnh=pnh, hasp=hasp, bid=r * nblk + n, tq=tq, p=p, vt_=vt_, vtb=vtb):
                                                    if hh == 0:
                                                        pnh["b"] = nextps()
                                                    pn, pnb = pnh["b"]
                                                    tp = (0, 64 * hh)
                                                    o_n = pn[64 * hh:64 * hh + 64, 0:128]
                                                    o_d = pn[64 * hh:64 * hh + 64, 128:256]
                                                    kb.mm(o_n, vt_[:, bid, 64 * hh:64 * hh + 64], pe_[:, 0:128], True, not hasp,
                                                          [vtb, peb], [pnb], tp=tp)
                                                    if hasp:
                                                        kb.mm(o_n, vt_[:, bid - 1, 64 * hh:64 * hh + 64], pe_[:, 128:256], False, True,
                                                              [vtb, peb], [pnb], tp=tp)
                                                    kb.mm(o_d, ones_b[:, 0:64], pe_[:, 0:128], True, not hasp, [cbf_b, peb], [pnb], tp=tp)
                                                    if hasp:
                                                        kb.mm(o_d, ones_b[:, 0:64], pe_[:, 128:256], False, True, [cbf_b, peb], [pnb], tp=tp)
                                                    if hh == 1:
                                                        src2 = pn[:, 0:256].rearrange("p (a t) -> p a t", a=2)
                                                        if p == 0:
                                                            kb.copy("dve", acc[:, :, tq], src2, [pnb], [acc_b])
                                                        else:
                                                            kb.tt("dve", acc[:, :, tq], acc[:, :, tq], src2, ALU.add, [acc_b, pnb], [acc_b])
                                                pend.append(pv)
                                                flush(3)
                                flush(0)
                                kb.recip(rden[:], acc[:, 1, :], [acc_b], [rden_b])
                                kb.tt("dve", oT[:, hp, :], acc[:, 0, :], rden[:], ALU.mult, [acc_b, rden_b], [oT_b])
                    S.barrier()

                    if 'B3' in PH:
                        with ExitStack() as st:
                            cwr, cwr_b = kb.sb(st, "cwr", [31, 512], F32)
                            kb.load(cwr[:], cwr_b, P["conv_w"][l])
                            cwT, cwT_b = kb.sb(st, "cwT", [128, 4, 31], F32)
                            cvp, cvp_b = kb.sb(st, "cvp", [128, 12], F32)
                            kb.load(cvp[:, 0:4], cvp_b, P["conv_b"][l].rearrange("(c p) -> p c", p=128), slow=True)
                            kb.load(cvp[:, 4:8], cvp_b, P["conv_ln_g"][l].rearrange("(c p) -> p c", p=128), slow=True)
                            kb.load(cvp[:, 8:12], cvp_b, P["conv_ln_b"][l].rearrange("(c p) -> p c", p=128), slow=True)
                            for c in range(4):
                                pt, pb = nextps()
                                kb.tr(pt[:, 0:31], cwr[:, c * 128:(c + 1) * 128], ident_f[0:31, 0:31], [cwr_b, cst_b], [pb])
                                kb.copy("dve", cwT[:, c, :], pt[:, 0:31], [pb], [cwT_b])
                            dg, dg_b = kb.sb(st, "dg", [128, 31, 128], BF16)
                            hpad, hpad_b = kb.sb(st, "hpad", [128, 30 + L], BF16)
                            sg_, sg_b = kb.sb(st, "sgl", [128, 512], F32)
                            cf, cf_b = kb.sb(st, "cf", [128, 4, L], F32)
                            sq2, sq2_b = kb.sb(st, "sq2", [128, 512], F32)
                            kb.memset("pool", hpad[:, 0:30], 0.0, [hpad_b])
                            for c in range(4):
                                for j in range(31):
                                    kb.ts("dve", dg[:, j, :], ident_f, cwT[:, c, j:j + 1], None, ALU.mult, None,
                                          [cst_b, cwT_b], [dg_b])
                                wa, wab = loadw("w_in", l, 8, 3072 + c * 128, 128)
                                wg, wgb = loadw("w_in", l, 8, 3584 + c * 128, 128)
                                for b in range(4):
                                    pa, pab = nextps()
                                    pg, pgb = nextps()
                                    for kc in range(8):
                                        kb.mm(pa[:], wa[:, kc, :], hT[:, kc, b * 512:(b + 1) * 512], kc == 0, kc == 7, [wab, hT_b], [pab])
                                    for kc in range(8):
                                        kb.mm(pg[:], wg[:, kc, :], hT[:, kc, b * 512:(b + 1) * 512], kc == 0, kc == 7, [wgb, hT_b], [pgb])
                                    kb.act(sg_[:], pg[:], AF.Sigmoid, [pgb], [sg_b])
                                    kb.tt("dve", hpad[:, 30 + b * 512:30 + (b + 1) * 512], pa[:], sg_[:], ALU.mult, [pab, sg_b], [hpad_b])
                                for b in range(4):
                                    pt, pb = nextps()
                                    for j in range(31):
                                        kb.mm(pt[:], dg[:, j, :], hpad[:, b * 512 + j:b * 512 + j + 512], j == 0, j == 30,
                                              [dg_b, hpad_b], [pb])
                                    kb.act(cf[:, c, b * 512:(b + 1) * 512], pt[:], AF.Identity, [pb, cvp_b], [cf_b], bias=cvp[:, c:c + 1])
                            mr, mr_b = kb.sb(st, "mr", [128, 2, 512], F32)
                            xc, xc_b = kb.sb(st, "xc", [128, 512], F32)
                            for b in range(4):
                                pm, pmb = nextps()
                                pv, pvb = nextps()
                                for c in range(4):
                                    kb.mm(pm[:], ones_f, cf[:, c, b * 512:(b + 1) * 512], c == 0, c == 3, [cst_b, cf_b], [pmb])
                                for c in range(4):
                                    kb.act(sq2[:], cf[:, c, b * 512:(b + 1) * 512], AF.Square, [cf_b], [sq2_b])
                                    kb.mm(pv[:], ones_f, sq2[:], c == 0, c == 3, [cst_b, sq2_b], [pvb])
                                kb.ts("dve", mr[:, 0, :], pm[:], 1.0 / 512, None, ALU.mult, None, [pmb], [mr_b])
                                kb.tt("dve", xc[:], mr[:, 0, :], mr[:, 0, :], ALU.mult, [mr_b], [xc_b])
                                kb.stt("dve", xc[:], pv[:], 1.0 / 512, xc[:], ALU.mult, ALU.subtract, [pvb, xc_b], [xc_b])
                                kb.ts("dve", xc[:], xc[:], 1e-6, None, ALU.add, None, [xc_b], [xc_b])
                                kb.act(xc[:], xc[:], AF.Sqrt, [xc_b], [xc_b])
                                kb.recip(mr[:, 1, :], xc[:], [xc_b], [mr_b])
                                for c in range(4):
                                    kb.tt("dve", xc[:], cf[:, c, b * 512:(b + 1) * 512], mr[:, 0, :], ALU.subtract, [cf_b, mr_b], [xc_b])
                                    kb.tt("dve", xc[:], xc[:], mr[:, 1, :], ALU.mult, [xc_b, mr_b], [xc_b])
                                    kb.act(hcT[:, c, b * 512:(b + 1) * 512], xc[:], AF.Silu, [xc_b, cvp_b], [hcT_b],
                                           bias=cvp[:, 8 + c:9 + c], scale=cvp[:, 4 + c:5 + c])
                    S.barrier()

                    if 'C' in PH:
                        with ExitStack() as st:
                            mg, mg_b = kb.sb(st, "mg", [128, 8, L], BF16)
                            bg, bg_b = kb.sb(st, "bg", [128, 24], F32)
                            kb.load(bg[:], bg_b, P["b_gate"][l].rearrange("(j p) -> p j", p=128), slow=True)
                            gs = [kb.sb(st, "gs%d" % i, [128, 512], F32) for i in range(3)]
                            ta, ta_b = kb.sb(st, "ta", [128, 512], F32)
                            tb2, tb2_b = kb.sb(st, "tb2", [128, 512], F32)
                            wcb = [kb.sb(st, "wcb%d" % i, [128, 1024], BF16) for i in range(7)]

                            def loadc(i, name, kc, c0):
                                t, b = wcb[i]
                                dst = t[:, 0:kc * 128].rearrange("p (k n) -> p k n", k=kc)
                                src = WB[name][l, :, c0:c0 + 128].rearrange("(k p) n -> p k n", p=128)
                                kb.load(dst, b, src, WBbuf[name][l])
                                return dst, b
                            for fc in range(8):
                                wg3 = [loadc(br, "w_in", 8, 4096 + br * 1024 + fc * 128) for br in range(3)]
                                wz1 = loadc(3, "w_ssm_glu", 4, fc * 128)
                                wz2 = loadc(4, "w_ssm_glu", 4, 1024 + fc * 128)
                                wau = loadc(5, "w_att_up", 4, fc * 128)
                                wpw = loadc(6, "w_conv_pw2", 4, fc * 128)
                                for b in range(4):
                                    bs = slice(b * 512, (b + 1) * 512)
                                    for br in range(3):
                                        pt, pb = nextps()
                                        for kc in range(8):
                                            kb.mm(pt[:], wg3[br][0][:, kc, :], hT[:, kc, bs], kc == 0, kc == 7, [wg3[br][1], hT_b], [pb])
                                        kb.act(gs[br][0][:], pt[:], AF.Sigmoid, [pb, bg_b], [gs[br][1]],
                                               bias=bg[:, br * 8 + fc:br * 8 + fc + 1])
                                    p1, p1b = nextps()
                                    p2, p2b = nextps()
                                    for kc in range(4):
                                        kb.mm(p1[:], wz1[0][:, kc, :], ysT[:, kc, bs], kc == 0, kc == 3, [wz1[1], ysT_b], [p1b])
                                    for kc in range(4):
                                        kb.mm(p2[:], wz2[0][:, kc, :], ysT[:, kc, bs], kc == 0, kc == 3, [wz2[1], ysT_b], [p2b])
                                    kb.act(ta[:], p2[:], AF.Sigmoid, [p2b], [ta_b])
                                    kb.tt("dve", ta[:], p1[:], ta[:], ALU.mult, [p1b, ta_b], [ta_b])
                                    kb.tt("dve", ta[:], ta[:], gs[0][0][:], ALU.mult, [ta_b, gs[0][1]], [ta_b])
                                    p3, p3b = nextps()
                                    for kc in range(4):
                                        kb.mm(p3[:], wau[0][:, kc, :], oT[:, kc, bs], kc == 0, kc == 3, [wau[1], oT_b], [p3b])
                                    kb.tt("dve", tb2[:], p3[:], gs[1][0][:], ALU.mult, [p3b, gs[1][1]], [tb2_b])
                                    kb.tt("pool", ta[:], ta[:], tb2[:], ALU.add, [ta_b, tb2_b], [ta_b])
                                    p4, p4b = nextps()
                                    for kc in range(4):
                                        kb.mm(p4[:], wpw[0][:, kc, :], hcT[:, kc, bs], kc == 0, kc == 3, [wpw[1], hcT_b], [p4b])
                                    kb.tt("dve", tb2[:], p4[:], gs[2][0][:], ALU.mult, [p4b, gs[2][1]], [tb2_b])
                                    kb.tt("dve", mg[:, fc, bs], ta[:], tb2[:], ALU.add, [ta_b, tb2_b], [mg_b])
                            wo = [loadw("w_out", l, 8, hf * 512, 512) for hf in range(2)]
                            xt = [kb.sb(st, "xc%d" % i, [128, D], F32) for i in range(2)]
                            for i in range(16):
                                xa, xab = xt[i % 2]
                                src, srcb = xsrc(i)
                                kb.load(xa[:], xab, src, srcb)
                                for hf in range(2):
                                    pt, pb = nextps()
                                    for kc in range(8):
                                        kb.mm(pt[:], mg[:, kc, i * 128:(i + 1) * 128], wo[hf][0][:, kc, :], kc == 0, kc == 7,
                                              [mg_b, wo[hf][1]], [pb])
                                    kb.tt("dve", xa[:, hf * 512:(hf + 1) * 512], xa[:, hf * 512:(hf + 1) * 512], pt[:], ALU.add,
                                          [xab, pb], [xab])
                                kb.store(xmid_d[i * 128:(i + 1) * 128, :], xmid_b[i], xa[:], xab)
                    S.barrier()

                if 'E' in PH:
                    with ExitStack() as st:
                        gbc, gbc_b = kb.sb(st, "g2bc", [128, D], F32)
                        kb.load(gbc[:], gbc_b, P["norm2_g"][l:l + 1, :].broadcast_to([128, D]))
                        if last:
                            gfc, gfc_b = kb.sb(st, "gfbc", [128, D], F32)
                            kb.load(gfc[:], gfc_b, final_g_d[0:1, :].broadcast_to([128, D]))
                        elif FUSEA:
                            gnx, gnx_b = kb.sb(st, "gnxbc", [128, D], F32)
                            kb.load(gnx[:], gnx_b, P["norm1_g"][l + 1:l + 2, :].broadcast_to([128, D]))
                            hbn = [kb.sb(st, "hbn%d" % i, [128, D], BF16) for i in range(2)]
                        xm2 = [[kb.sb(st, "xm%d_%d" % (q, i), [128, D], F32) for i in range(4)] for q in range(2)]
                        sqt, sqt_b = kb.sb(st, "sqtE", [128, D], F32)
                        hb = [kb.sb(st, "hbE%d" % i, [128, D], BF16) for i in range(4)]
                        sm = [kb.sb(st, "smE%d" % i, [128, 4], F32) for i in range(2)]
                        h2T2 = [kb.sb(st, "h2T%d" % q, [128, 8, 512], BF16) for q in range(2)]
                        gT, gT_b = kb.sb(st, "gT", [128, 22, 512], BF16)
                        wfo = [kb.sb(st, "wfo%d" % i, [128, 22, 512], BF16) for i in range(2)]
                        yo = [kb.sb(st, "yo%d" % i, [128, D], F32) for i in range(2)]
                        wfi = [0]

                        def norm_p1(b):
                            for j in range(4):
                                i = b * 4 + j
                                xa, xab = xm2[b % 2][j]
                                kb.load(xa[:], xab, xmid_d[i * 128:(i + 1) * 128, :], xmid_b[i])
                                kb.act(sqt[:], xa[:], AF.Square, [xab], [sqt_b])
                                s_, s_b = sm[i % 2]
                                kb.rsum(s_[:, 0:1], sqt[:], [sqt_b], [s_b])
                                kb.ts("dve", s_[:, 1:2], s_[:, 0:1], 1.0 / D, 1e-6, ALU.mult, ALU.add, [s_b], [s_b])
                                kb.act(s_[:, 2:3], s_[:, 1:2], AF.Sqrt, [s_b], [s_b])
                                kb.recip(s_[:, 3:4], s_[:, 2:3], [s_b], [s_b])
                                ha, hab = hb[j]
                                kb.stt("dve", ha[:], xa[:], s_[:, 3:4], gbc[:], ALU.mult, ALU.mult, [xab, s_b, gbc_b], [hab])

                        def norm_p2(b):
                            h2T, h2T_b = h2T2[b % 2]
                            for j in range(4):
                                ha, hab = hb[j]
                                pt, pb = nextps()
                                ptb = pt[:].bitcast(BF16)
                                for c in range(8):
                                    kb.tr(ptb[:, c * 128:(c + 1) * 128], ha[:, c * 128:(c + 1) * 128], ident_b, [hab, cbf_b], [pb])
                                kb.copy("act", h2T[:, :, j * 128:(j + 1) * 128],
                                        ptb[:, 0:1024].rearrange("p (c t) -> p c t", c=8), [pb], [h2T_b])

                        def ffn_in(b):
                            h2T, h2T_b = h2T2[b % 2]
                            for t in range(11):
                                wv, wb_ = loadw("w_ffn_in", l, 8, t * 512, 512)
                                for cq in range(4):
                                    cid = 4 * t + cq
                                    pt, pb = nextps()
                                    for kc in range(8):
                                        kb.mm(pt[:], wv[:, kc, cq * 128:(cq + 1) * 128], h2T[:, kc, :], kc == 0, kc == 7,
                                              [wb_, h2T_b], [pb])
                                    if cid < 22:
                                        kb.act(gT[:, cid, :], pt[:], AF.Silu, [pb], [gT_b])
                                    else:
                                        kb.tt("dve", gT[:, cid - 22, :], gT[:, cid - 22, :], pt[:], ALU.mult, [gT_b, pb], [gT_b])

                        def ffn_out_q(b, qd):
                            if qd % 2:
                                return
                            hf = qd // 2
                            wt, wtb = wfo[wfi[0] % 2]
                            wfi[0] += 1
                            kb.load(wt[:], wtb, WB["w_ffn_out"][l, :, hf * 512:(hf + 1) * 512].rearrange("(k p) n -> p k n", p=128),
                                    WBbuf["w_ffn_out"][l])
                            for j in range(4):
                                xa, xab = xm2[b % 2][j]
                                pt, pb = nextps()
                                for kc in range(22):
                                    kb.mm(pt[:], gT[:, kc, j * 128:(j + 1) * 128], wt[:, kc, :], kc == 0, kc == 21,
                                          [gT_b, wtb], [pb])
                                kb.tt("dve", xa[:, hf * 512:(hf + 1) * 512], xa[:, hf * 512:(hf + 1) * 512], pt[:], ALU.add,
                                      [xab, pb], [xab])
                        norm_p1(0)
                        norm_p2(0)
                        for b in range(4):
                            xm = xm2[b % 2]
                            ffn_in(b)
                            if b + 1 < 4:
                                norm_p1(b + 1)
                            ffn_out_q(b, 0)
                            ffn_out_q(b, 1)
                            if b + 1 < 4:
                                norm_p2(b + 1)
                            ffn_out_q(b, 2)
                            ffn_out_q(b, 3)
                            for j in range(4):
                                i = b * 4 + j
                                xa, xab = xm[j]
                                if not last:
                                    kb.store(xout_d[i * 128:(i + 1) * 128, :], xout_b[i], xa[:], xab)
                                    if FUSEA:
                                        kb.act(sqt[:], xa[:], AF.Square, [xab], [sqt_b])
                                        s_, s_b = sm[i % 2]
                                        kb.rsum(s_[:, 0:1], sqt[:], [sqt_b], [s_b])
                                        kb.ts("dve", s_[:, 1:2], s_[:, 0:1], 1.0 / D, 1e-6, ALU.mult, ALU.add, [s_b], [s_b])
                                        kb.act(s_[:, 2:3], s_[:, 1:2], AF.Sqrt, [s_b], [s_b])
                                        kb.recip(s_[:, 3:4], s_[:, 2:3], [s_b], [s_b])
                                        ha, hab = hbn[i % 2]
                                        kb.stt("dve", ha[:], xa[:], s_[:, 3:4], gnx[:], ALU.mult, ALU.mult, [xab, s_b, gnx_b], [hab])
                                        pt, pb = nextps()
                                        ptb = pt[:].bitcast(BF16)
                                        for c in range(8):
                                            kb.tr(ptb[:, c * 128:(c + 1) * 128], ha[:, c * 128:(c + 1) * 128], ident_b, [hab, cbf_b], [pb])
                                        kb.copy("act", hT[:, :, i * 128:(i + 1) * 128],
                                                ptb[:, 0:1024].rearrange("p (c t) -> p c t", c=8), [pb], [hT_b])
                                else:
                                    kb.act(sqt[:], xa[:], AF.Square, [xab], [sqt_b])
                                    s_, s_b = sm[i % 2]
                                    kb.rsum(s_[:, 0:1], sqt[:], [sqt_b], [s_b])
                                    kb.ts("dve", s_[:, 1:2], s_[:, 0:1], 1.0 / D, 1e-6, ALU.mult, ALU.add, [s_b], [s_b])
                                    kb.act(s_[:, 2:3], s_[:, 1:2], AF.Sqrt, [s_b], [s_b])
                                    kb.recip(s_[:, 3:4], s_[:, 2:3], [s_b], [s_b])
                                    ya, yab = yo[i % 2]
                                    kb.stt("dve", ya[:], xa[:], s_[:, 3:4], gfc[:], ALU.mult, ALU.mult, [xab, s_b, gfc_b], [yab])
                                    kb.store(out_d[sq * L + i * 128: sq * L + (i + 1) * 128, :], out_b, ya[:], yab)
                S.barrier()
        S.barrier(full=True)
        S.run(st0)
    return nc


def make_consts():
    c = np.zeros((128, NCONST), np.float32)
    i = np.arange(128)
    c[:, 0:128] = np.eye(128)
    j = np.zeros((128, 128), np.float32)
    j[i[:64], i[:64] + 64] = 1.0
    j[i[:64] + 64, i[:64]] = 1.0
    c[:, 128:256] = j
    k = i[:, None]
    q = i[None, :]
    c[:, 256:384] = np.where(k <= q, 0.0, NEG)
    c[:, 384:512] = np.where(k >= q, 0.0, NEG)
    c[:, 512:640] = 1.0
    c[:, 640:648] = (i[:, None] // 16 == np.arange(8)[None, :]).astype(np.float32)
    c[:64, 648] = -1.0
    c[64:, 648] = 1.0
    return c


_NC_CACHE = {}


def run(inputs, NSEQ, DEPTH, ncores=8):
    key = (NSEQ, DEPTH)
    if key not in _NC_CACHE:
        _NC_CACHE[key] = build(NSEQ, DEPTH)
    nc = _NC_CACHE[key]
    consts = make_consts()
    x = np.asarray(inputs["x"], np.float32)
    in_maps = []
    for c in range(ncores):
        m = {"x": np.ascontiguousarray(x[c * NSEQ:(c + 1) * NSEQ].reshape(NSEQ * L, D)),
             "consts": consts,
             "final_g": np.ascontiguousarray(np.asarray(inputs["final_g"], np.float32).reshape(1, D))}
        for n, shp in PNAMES:
            m[n] = np.ascontiguousarray(np.asarray(inputs[n], np.float32)[:DEPTH])
        in_maps.append(m)
    res = run_bass_kernel_spmd(nc, in_maps, core_ids=list(range(ncores)))
    outs = [r["out"].reshape(NSEQ, L, D) for r in res.results]
    return np.concatenate(outs, axis=0).astype(np.float32)


def kernel(**inputs):
    return run(inputs, 4, 2, 8)
```

```python
import math
import os
from contextlib import ExitStack
import numpy as np
import concourse.bass as bass
import concourse.mybir as mybir
from concourse.bass_utils import run_bass_kernel_spmd

F32 = mybir.dt.float32
BF16 = mybir.dt.bfloat16
AF = mybir.ActivationFunctionType
ALU = mybir.AluOpType
AX = mybir.AxisListType

L = 2048
D = 1024
DFF = 2816
NEG = -30000.0
NCONST = 128 * 5 + 8 + 1
PAD = 1024


class Buf:
    __slots__ = ("lw", "rd", "ds", "dso", "excl")

    def __init__(self, excl=False):
        self.excl = excl
        self.lw = None
        self.rd = {}
        self.ds = None
        self.dso = None


class Sched:
    ENG = ("pe", "act", "dve", "pool", "sp")

    def __init__(self, nc):
        self.nc = nc
        self.ops = {e: [] for e in self.ENG}
        self.cnt = {e: 0 for e in self.ENG}
        self.waited = {e: {} for e in self.ENG}
        self.dcnt = {}
        self.nobar = set()

    def _deps(self, reads, writes):
        deps = {}
        for b in reads:
            if b.lw is not None and deps.get(b.lw[0], 0) < b.lw[1]:
                deps[b.lw[0]] = b.lw[1]
            if b.excl:
                for k, v in b.rd.items():
                    if deps.get(k, 0) < v:
                        deps[k] = v
        for b in writes:
            if b.lw is not None and deps.get(b.lw[0], 0) < b.lw[1]:
                deps[b.lw[0]] = b.lw[1]
            for k, v in b.rd.items():
                if deps.get(k, 0) < v:
                    deps[k] = v
        return deps

    def _waits(self, eng, deps, skip_self):
        w = []
        wd = self.waited[eng]
        for k, v in deps.items():
            if skip_self and k == eng:
                continue
            if wd.get(k, 0) >= v:
                continue
            wd[k] = v
            w.append((k, v))
        return w

    def _fin(self, ev, reads, writes):
        k, v = ev
        for b in reads:
            if b.rd.get(k, 0) < v:
                b.rd[k] = v
        for b in writes:
            b.lw = ev
            b.rd = {}

    def op(self, eng, fn, reads=(), writes=()):
        w = self._waits(eng, self._deps(reads, writes), eng == "pe")
        self.cnt[eng] += 1
        ev = (eng, self.cnt[eng])
        self.ops[eng].append((w, fn, (eng, 1)))
        self._fin(ev, reads, writes)

    def dma(self, q, dsem, fn, reads=(), writes=()):
        w = self._waits(q, self._deps(reads, writes), False)
        key = ("d", dsem)
        self.dcnt[key] = self.dcnt.get(key, 0) + 16
        ev = (key, self.dcnt[key])
        self.ops[q].append((w, fn, (key, 16)))
        self._fin(ev, reads, writes)

    def barrier(self, full=False):
        for e in self.ENG:
            deps = {k: v for k, v in self.cnt.items() if v > 0}
            deps.update({k: v for k, v in self.dcnt.items() if full or k not in self.nobar})
            w = self._waits(e, deps, True)
            if w:
                self.ops[e].append((w, None, None))

    def run(self, stack):
        nc = self.nc
        sems = {}
        for e in self.ENG:
            sems[e] = stack.enter_context(nc.semaphore("s_" + e))
        for key in self.dcnt:
            sems[key] = stack.enter_context(nc.semaphore("d%d" % key[1]))
        block = stack.enter_context(nc.Block())
        ops = self.ops

        def replay(e, eng):
            for w, fn, inc in ops[e]:
                for k, v in w:
                    eng.wait_ge(sems[k], v)
                if fn is not None:
                    fn(eng).then_inc(sems[inc[0]], inc[1])

        @block.tensor
        def _(eng):
            replay("pe", eng)

        @block.scalar
        def _(eng):
            replay("act", eng)

        @block.vector
        def _(eng):
            replay("dve", eng)

        @block.gpsimd
        def _(eng):
            replay("pool", eng)

        @block.sync
        def _(eng):
            replay("sp", eng)


class KB:
    def __init__(self, nc):
        self.nc = nc
        self.S = Sched(nc)
        self.nds = 0

    def sb(self, st, name, shape, dt):
        self.nsb = getattr(self, "nsb", 0) + 1
        if not hasattr(self, "bufs"):
            self.bufs = {}
        b = self.bufs.get(name)
        if b is None:
            b = self.bufs[name] = Buf()
        return st.enter_context(self.nc.sbuf_tensor("%s_u%d" % (name, self.nsb), shape, dt)), b

    def load(self, dst, dbuf, src, sbuf=None, q="sp", slow=False):
        if dbuf.ds is None:
            dbuf.ds = self.nds
            self.nds += 1
        kw = {"allow_slow_non_contiguous": True} if slow else {}
        self.S.dma(q, dbuf.ds, lambda e: e.dma_start(out=dst, in_=src, **kw),
                   reads=[sbuf] if sbuf is not None else [], writes=[dbuf])

    def store(self, dst, dbuf, src, sbuf, q="sp"):
        if sbuf.dso is None:
            sbuf.dso = self.nds
            self.nds += 1
        self.S.dma(q, sbuf.dso, lambda e: e.dma_start(out=dst, in_=src), reads=[sbuf], writes=[dbuf])

    def mm(self, out, lhsT, rhs, start, stop, reads, writes, tp=None):
        kw = {} if tp is None else {"tile_position": tp}
        self.S.op("pe", lambda e: e.matmul(out, lhsT=lhsT, rhs=rhs, start=start, stop=stop, **kw), reads, writes)

    def tr(self, out, in_, ident, reads, writes):
        self.S.op("pe", lambda e: e.transpose(out=out, in_=in_, identity=ident), reads, writes)

    def act(self, out, in_, func, reads, writes, bias=None, scale=None):
        kw = {}
        if bias is not None:
            kw["bias"] = bias
        if scale is not None:
            kw["scale"] = scale
        self.S.op("act", lambda e: e.activation(out=out, in_=in_, func=func, **kw), reads, writes)

    def tt(self, eng, out, in0, in1, op, reads, writes):
        self.S.op(eng, lambda e: e.tensor_tensor(out=out, in0=in0, in1=in1, op=op), reads, writes)

    def ts(self, eng, out, in0, s1, s2, op0, op1, reads, writes):
        if s2 is None:
            self.S.op(eng, lambda e: e.tensor_scalar(out=out, in0=in0, scalar1=s1, scalar2=None, op0=op0), reads, writes)
        else:
            self.S.op(eng, lambda e: e.tensor_scalar(out=out, in0=in0, scalar1=s1, scalar2=s2, op0=op0, op1=op1),
                      reads, writes)

    def stt(self, eng, out, in0, scalar, in1, op0, op1, reads, writes):
        self.S.op(eng, lambda e: e.scalar_tensor_tensor(out=out, in0=in0, scalar=scalar, in1=in1, op0=op0, op1=op1),
                  reads, writes)

    def copy(self, eng, out, in_, reads, writes):
        if eng == "act":
            self.act(out, in_, AF.Copy, reads, writes)
        else:
            self.S.op(eng, lambda e: e.tensor_copy(out=out, in_=in_), reads, writes)

    def memset(self, eng, ap, val, writes):
        self.S.op(eng, lambda e: e.memset(ap, val), (), writes)

    def recip(self, out, in_, reads, writes):
        self.S.op("dve", lambda e: e.reciprocal(out=out, in_=in_), reads, writes)

    def rsum(self, out, in_, reads, writes):
        self.S.op("dve", lambda e: e.reduce_sum(out=out, in_=in_, axis=AX.X), reads, writes)


PNAMES = [
    ("norm1_g", (D,)), ("w_in", (D, 7168)), ("b_gate", (3072,)), ("ssm_lambda_re", (32, 64)),
    ("ssm_lambda_im", (32, 64)), ("ssm_log_dt", (32,)), ("ssm_b_re", (32, 64, 16)), ("ssm_b_im", (32, 64, 16)),
    ("ssm_c_re", (32, 16, 64)), ("ssm_c_im", (32, 16, 64)), ("ssm_d", (512,)), ("w_ssm_glu", (512, 2048)),
    ("w_att_up", (512, D)), ("conv_w", (31, 512)), ("conv_b", (512,)), ("conv_ln_g", (512,)),
    ("conv_ln_b", (512,)), ("w_conv_pw2", (512, D)), ("w_out", (D, D)), ("norm2_g", (D,)),
    ("w_ffn_in", (D, 2 * DFF)), ("w_ffn_out", (DFF, D)),
]
BIGW = ["w_in", "w_ssm_glu", "w_att_up", "w_conv_pw2", "w_out", "w_ffn_in", "w_ffn_out"]
PATTERNS = ((128, 1), (512, 4), (2048, 16))
PSTOP = int(os.environ.get('PSTOP', '9'))
B1STOP = int(os.environ.get('B1STOP', '9'))
B1VAR = os.environ.get('B1VAR', '')
S5ID = int(os.environ.get('S5ID', '2'))
FUSEA = int(os.environ.get('FUSEA', '0'))
PH = set(os.environ.get('KPH', 'prep,A,B1,B2,B3,C,E').split(','))


def build(NSEQ, DEPTH):
    nc = bass.Bass("TRN2", target_bir_lowering=False)
    kb = KB(nc)
    S = kb.S
    x_d = nc.dram_tensor("x", [NSEQ * L, D], F32, kind="ExternalInput").ap()
    out_d = nc.dram_tensor("out", [NSEQ * L, D], F32, kind="ExternalOutput").ap()
    consts_d = nc.dram_tensor("consts", [128, NCONST], F32, kind="ExternalInput").ap()
    final_g_d = nc.dram_tensor("final_g", [1, D], F32, kind="ExternalInput").ap()
    P = {}
    for n, shp in PNAMES:
        P[n] = nc.dram_tensor(n, [DEPTH] + list(shp), F32, kind="ExternalInput").ap()
    WB = {}
    WBbuf = {}
    for n in BIGW:
        shp = dict(PNAMES)[n]
        WB[n] = nc.dram_tensor(n + "_bf", [DEPTH] + list(shp), BF16, kind="Internal").ap()
        WBbuf[n] = [Buf() for _ in range(DEPTH)]
    qpow_d = nc.dram_tensor("qpow", [DEPTH, 32, 128, 16 * 128], BF16, kind="Internal").ap()
    qpow_b = [Buf() for _ in range(DEPTH)]
    bst_d = nc.dram_tensor("bst_d", [DEPTH, 128, 4096], BF16, kind="Internal").ap()
    cst_d = nc.dram_tensor("cst_d", [DEPTH, 128, 4096], BF16, kind="Internal").ap()
    bc_b = [Buf() for _ in range(DEPTH)]
    xmid_d = nc.dram_tensor("xmid", [L, D], F32, kind="Internal").ap()
    xout_d = nc.dram_tensor("xout", [L, D], F32, kind="Internal").ap()
    xmid_b = [Buf() for _ in range(16)]
    xout_b = [Buf() for _ in range(16)]
    out_b = Buf()

    with ExitStack() as st0:
        for l in range(DEPTH):
            for n in BIGW:
                K = dict(PNAMES)[n][0]
                for r in range(0, K, 128):
                    kb.load(WB[n][l, r:r + 128, :], WBbuf[n][l], P[n][l, r:r + 128, :], q="pool")
                S.nobar.add(("d", WBbuf[n][l].ds))

        cst, cst_b = kb.sb(st0, "cst", [128, NCONST], F32)
        kb.load(cst[:], cst_b, consts_d)
        ident_f = cst[:, 0:128]
        jswap_f = cst[:, 128:256]
        ones_f = cst[:, 512:640]
        rowmask = cst[:, 640:648]
        sgn = cst[:, 648:649]
        cbf, cbf_b = kb.sb(st0, "cbf", [128, 640], BF16)
        kb.copy("dve", cbf[:], cst[:, 0:640], [cst_b], [cbf_b])
        ident_b = cbf[:, 0:128]
        mcur_b = cbf[:, 256:384]
        mprev_b = cbf[:, 384:512]
        ones_b = cbf[:, 512:640]

        psb = []
        for i in range(8):
            t = st0.enter_context(nc.psum_tensor("ps%d" % i, [128, 512], F32))
            psb.append((t, Buf(excl=True)))
        psi = [0]

        def nextps(pool=(0, 1, 2, 3, 4, 5, 6, 7)):
            psi[0] += 1
            return psb[pool[psi[0] % len(pool)]]

        NSLOT = 4
        ring = [kb.sb(st0, "wr%d" % i, [128, 4096], BF16) for i in range(NSLOT)]
        ri = [0]

        def loadw(name, l, kc, c0, ncols):
            t, b = ring[ri[0] % NSLOT]
            ri[0] += 1
            dst = t[:, 0:kc * ncols].rearrange("p (k n) -> p k n", k=kc)
            src = WB[name][l, :, c0:c0 + ncols].rearrange("(k p) n -> p k n", p=128)
            kb.load(dst, b, src, WBbuf[name][l])
            return dst, b

        s5c = []
        for l in (range(DEPTH) if 'prep' in PH else ()):
            dsk, dsk_b = kb.sb(st0, "dsk%d" % l, [128, 4], F32)
            s5c.append((dsk, dsk_b))
            with ExitStack() as st:
                Bst, Bst_b = kb.sb(st, "Bst%d" % l, [128, 32, 128], BF16)
                Cst, Cst_b = kb.sb(st, "Cst%d" % l, [128, 32, 128], BF16)
                def T(name, w, dt=F32):
                    return kb.sb(st, "%s_%d" % (name, l), [128, w], dt)
                lr, lr_b = T("lr", 32)
                li, li_b = T("li", 32)
                ldt, ldt_b = T("ldt", 32)
                for h in (0, 64):
                    kb.load(lr[h:h + 64, :], lr_b, P["ssm_lambda_re"][l].rearrange("g p -> p g"), slow=True)
                    kb.load(li[h:h + 64, :], li_b, P["ssm_lambda_im"][l].rearrange("g p -> p g"), slow=True)
                kb.load(ldt[:], ldt_b, P["ssm_log_dt"][l:l + 1, :].broadcast_to([128, 32]))
                X1, X1_b = T("X1", 512)
                X2, X2_b = T("X2", 512)
                bre = P["ssm_b_re"][l].rearrange("g p h -> p g h")
                bim = P["ssm_b_im"][l].rearrange("g p h -> p g h")
                X1v = X1[:].rearrange("p (g h) -> p g h", h=16)
                X2v = X2[:].rearrange("p (g h) -> p g h", h=16)
                kb.load(X1v[0:64], X1_b, bre)
                kb.load(X1v[64:128], X1_b, bim)
                kb.load(X2v[0:64], X2_b, bim)
                kb.load(X2v[64:128], X2_b, bre)
                CT, CT_b = T("CT", 512)
                for c in range(4):
                    kb.load(CT[:, c * 128:c * 128 + 64], CT_b,
                            P["ssm_c_re"][l, 8 * c:8 * c + 8].rearrange("g h p -> (g h) p"))
                    kb.load(CT[:, c * 128 + 64:c * 128 + 128], CT_b,
                            P["ssm_c_im"][l, 8 * c:8 * c + 8].rearrange("g h p -> (g h) p"))
                kb.load(dsk[:], dsk_b, P["ssm_d"][l].rearrange("(c p) -> p c", p=128), slow=True)
                dt_, dt_b = T("dt", 32)
                kb.act(dt_[:], ldt[:], AF.Exp, [ldt_b], [dt_b])
                t1, t1_b = T("t1", 32)
                t2, t2_b = T("t2", 32)
                t3, t3_b = T("t3", 32)
                mag, mag_b = T("mag", 32)
                cc, cc_b = T("cc", 32)
                ss, ss_b = T("ss", 32)
                kb.tt("dve", t1[:], lr[:], dt_[:], ALU.mult, [lr_b, dt_b], [t1_b])
                kb.act(mag[:], t1[:], AF.Exp, [t1_b], [mag_b])
                kb.tt("dve", t2[:], li[:], dt_[:], ALU.mult, [li_b, dt_b], [t2_b])
                kb.ts("dve", t2[:], t2[:], 1.0 / 16.0, None, ALU.mult, None, [t2_b], [t2_b])
                kb.act(ss[:], t2[:], AF.Sin, [t2_b], [ss_b])
                kb.ts("dve", t3[:], t2[:], math.pi / 2, None, ALU.add, None, [t2_b], [t3_b])
                kb.act(cc[:], t3[:], AF.Sin, [t3_b], [cc_b])
                for _ in range(4):
                    kb.tt("dve", t1[:], cc[:], cc[:], ALU.mult, [cc_b], [t1_b])
                    kb.tt("dve", t2[:], ss[:], ss[:], ALU.mult, [ss_b], [t2_b])
                    kb.tt("dve", t3[:], cc[:], ss[:], ALU.mult, [cc_b, ss_b], [t3_b])
                    kb.tt("dve", cc[:], t1[:], t2[:], ALU.subtract, [t1_b, t2_b], [cc_b])
                    kb.ts("dve", ss[:], t3[:], 2.0, None, ALU.mult, None, [t3_b], [ss_b])
                ar, ar_b = T("ar", 32)
                ai, ai_b = T("ai", 32)
                kb.tt("dve", ar[:], mag[:], cc[:], ALU.mult, [mag_b, cc_b], [ar_b])
                kb.tt("dve", ai[:], mag[:], ss[:], ALU.mult, [mag_b, ss_b], [ai_b])
                nr, nr_b = T("nr", 32)
                kb.ts("dve", nr[:], ar[:], -1.0, None, ALU.add, None, [ar_b], [nr_b])
                den, den_b = T("den", 32)
                kb.tt("dve", t1[:], lr[:], lr[:], ALU.mult, [lr_b], [t1_b])
                kb.tt("dve", t2[:], li[:], li[:], ALU.mult, [li_b], [t2_b])
                kb.tt("dve", den[:], t1[:], t2[:], ALU.add, [t1_b, t2_b], [den_b])
                kb.recip(den[:], den[:], [den_b], [den_b])
                zr, zr_b = T("zr", 32)
                zi, zi_b = T("zi", 32)
                kb.tt("dve", t1[:], nr[:], lr[:], ALU.mult, [nr_b, lr_b], [t1_b])
                kb.tt("dve", t2[:], ai[:], li[:], ALU.mult, [ai_b, li_b], [t2_b])
                kb.tt("dve", t3[:], t1[:], t2[:], ALU.add, [t1_b, t2_b], [t3_b])
                kb.tt("dve", zr[:], t3[:], den[:], ALU.mult, [t3_b, den_b], [zr_b])
                kb.tt("dve", t1[:], ai[:], lr[:], ALU.mult, [ai_b, lr_b], [t1_b])
                kb.tt("dve", t2[:], nr[:], li[:], ALU.mult, [nr_b, li_b], [t2_b])
                kb.tt("dve", t3[:], t1[:], t2[:], ALU.subtract, [t1_b, t2_b], [t3_b])
                kb.tt("dve", zi[:], t3[:], den[:], ALU.mult, [t3_b, den_b], [zi_b])
                kb.ts("dve", zi[:], zi[:], sgn, None, ALU.mult, None, [zi_b, cst_b], [zi_b])
                zrb = zr[:].unsqueeze(2).broadcast_to([128, 32, 16])
                zib = zi[:].unsqueeze(2).broadcast_to([128, 32, 16])
                kb.tt("dve", X1v, X1v, zrb, ALU.mult, [X1_b, zr_b], [X1_b])
                kb.tt("dve", X2v, X2v, zib, ALU.mult, [X2_b, zi_b], [X2_b])
                kb.tt("dve", X1[:], X1[:], X2[:], ALU.add, [X1_b, X2_b], [X1_b])
                kb.memset("dve", Cst[:], 0.0, [Cst_b])
                for c in (range(4) if PSTOP >= 2 else ()):
                    pt, pb = nextps()
                    kb.tr(pt[:, 0:128], X1[:, c * 128:(c + 1) * 128], ident_f, [X1_b, cst_b], [pb])
                    for gi in range(8):
                        kb.ts("dve", Bst[:, 8 * c + gi, :], pt[:, 0:128], rowmask[:, gi:gi + 1], None, ALU.mult, None,
                              [pb, cst_b], [Bst_b])
                    kb.ts("dve", CT[:, c * 128 + 64:c * 128 + 128], CT[:, c * 128 + 64:c * 128 + 128], -1.0, None,
                          ALU.mult, None, [CT_b], [CT_b])
                    pt2, pb2 = nextps()
                    kb.tr(pt2[:, 0:128], CT[:, c * 128:(c + 1) * 128], ident_f, [CT_b, cst_b], [pb2])
                    for gi in range(8):
                        kb.copy("act", Cst[:, 8 * c + gi, gi * 16:(gi + 1) * 16], pt2[:, gi * 16:(gi + 1) * 16],
                                [pb2], [Cst_b])
                NQ = 16
                arK, arK_b = T("arK", NQ * 32)
                aiK, aiK_b = T("aiK", NQ * 32)
                c2K, c2K_b = T("c2K", NQ * 32)

                def qs(q):
                    return slice(q * 32, (q + 1) * 32)

                def cmul(qo, qa_, qb_):
                    xr, xi = arK[:, qs(qa_)], aiK[:, qs(qa_)]
                    yr, yi = arK[:, qs(qb_)], aiK[:, qs(qb_)]
                    kb.tt("dve", t1[:], xr, yr, ALU.mult, [arK_b], [t1_b])
                    kb.tt("dve", t2[:], xi, yi, ALU.mult, [aiK_b], [t2_b])
                    kb.tt("dve", arK[:, qs(qo)], t1[:], t2[:], ALU.subtract, [t1_b, t2_b], [arK_b])
                    kb.tt("dve", t1[:], xr, yi, ALU.mult, [arK_b, aiK_b], [t1_b])
                    kb.tt("dve", t2[:], xi, yr, ALU.mult, [arK_b, aiK_b], [t2_b])
                    kb.tt("dve", aiK[:, qs(qo)], t1[:], t2[:], ALU.add, [t1_b, t2_b], [aiK_b])
                kb.copy("dve", arK[:, 0:32], ar[:], [ar_b], [arK_b])
                kb.copy("dve", aiK[:, 0:32], ai[:], [ai_b], [aiK_b])
                for k in range(5):
                    cmul(3 * k + 1, 3 * k, 3 * k)
                    cmul(3 * k + 2, 3 * k + 1, 3 * k)
                    cmul(3 * k + 3, 3 * k + 1, 3 * k + 1)
                kb.ts("dve", c2K[:], aiK[:], sgn, -1.0, ALU.mult, ALU.mult, [aiK_b, cst_b], [c2K_b])
                stg = [kb.sb(st, "stg%d_%d" % (i, l), [128, NQ * 128], BF16) for i in range(2)]
                stmp = [kb.sb(st, "stmp%d_%d" % (i, l), [128, NQ * 128], BF16) for i in range(2)]
                idb = ident_f.unsqueeze(1).broadcast_to([128, NQ, 128])
                jsb = jswap_f.unsqueeze(1).broadcast_to([128, NQ, 128])
                for g in (range(32) if PSTOP >= 3 else ()):
                    sg, sg_b = stg[g % 2]
                    tm, tm_b = stmp[g % 2]
                    sgv = sg[:].rearrange("p (k n) -> p k n", k=NQ)
                    tmv = tm[:].rearrange("p (k n) -> p k n", k=NQ)
                    arv = arK[:].rearrange("p (k g) -> p k g", g=32)[:, :, g:g + 1].broadcast_to([128, NQ, 128])
                    c2v = c2K[:].rearrange("p (k g) -> p k g", g=32)[:, :, g:g + 1].broadcast_to([128, NQ, 128])
                    kb.tt("dve", tmv, jsb, c2v, ALU.mult, [cst_b, c2K_b], [tm_b])
                    kb.tt("dve", sgv, idb, arv, ALU.mult, [cst_b, arK_b], [sg_b])
                    kb.tt("dve", sg[:], sg[:], tm[:], ALU.add, [sg_b, tm_b], [sg_b])
                    kb.store(qpow_d[l, g], qpow_b[l], sg[:], sg_b)
                if PSTOP >= 4:
                    kb.store(bst_d[l], bc_b[l], Bst[:].rearrange("p g n -> p (g n)"), Bst_b)
                    kb.store(cst_d[l], bc_b[l], Cst[:].rearrange("p g n -> p (g n)"), Cst_b)
            S.barrier()

        if FUSEA:
            hT, hT_b = kb.sb(st0, "hT", [128, 8, L], BF16)
        for sq in range(NSEQ):
            for l in range(DEPTH):
                last = (l == DEPTH - 1)
                dsk, dsk_b = s5c[l] if s5c else (None, None)

                def xsrc(i):
                    if l == 0:
                        return x_d[sq * L + i * 128: sq * L + (i + 1) * 128, :], None
                    return xout_d[i * 128:(i + 1) * 128, :], xout_b[i]

                with ExitStack() as stB:
                    if not FUSEA:
                        hT, hT_b = kb.sb(stB, "hT", [128, 8, L], BF16)
                    ysT, ysT_b = kb.sb(stB, "ysT", [128, 4, L], BF16)
                    oT, oT_b = kb.sb(stB, "oT", [128, 4, L], BF16)
                    hcT, hcT_b = kb.sb(stB, "hcT", [128, 4, L], BF16)
                    if 'A' in PH and (l == 0 or not FUSEA):
                        with ExitStack() as st:
                            gbc, gbc_b = kb.sb(st, "gbc", [128, D], F32)
                            kb.load(gbc[:], gbc_b, P["norm1_g"][l:l + 1, :].broadcast_to([128, D]))
                            xt = [kb.sb(st, "xt%d" % i, [128, D], F32) for i in range(2)]
                            sqt, sqt_b = kb.sb(st, "sqt", [128, D], F32)
                            hb = [kb.sb(st, "hb%d" % i, [128, D], BF16) for i in range(2)]
                            sm = [kb.sb(st, "sm%d" % i, [128, 4], F32) for i in range(2)]
                            for i in range(16):
                                xa, xb_ = xt[i % 2]
                                src, srcb = xsrc(i)
                                kb.load(xa[:], xb_, src, srcb)
                                kb.act(sqt[:], xa[:], AF.Square, [xb_], [sqt_b])
                                s_, s_b = sm[i % 2]
                                kb.rsum(s_[:, 0:1], sqt[:], [sqt_b], [s_b])
                                kb.ts("dve", s_[:, 1:2], s_[:, 0:1], 1.0 / D, 1e-6, ALU.mult, ALU.add, [s_b], [s_b])
                                kb.act(s_[:, 2:3], s_[:, 1:2], AF.Sqrt, [s_b], [s_b])
                                kb.recip(s_[:, 3:4], s_[:, 2:3], [s_b], [s_b])
                                ha, hab = hb[i % 2]
                                kb.stt("dve", ha[:], xa[:], s_[:, 3:4], gbc[:], ALU.mult, ALU.mult, [xb_, s_b, gbc_b], [hab])
                                pt, pb = nextps()
                                ptb = pt[:].bitcast(BF16)
                                for c in range(8):
                                    kb.tr(ptb[:, c * 128:(c + 1) * 128], ha[:, c * 128:(c + 1) * 128], ident_b, [hab, cbf_b], [pb])
                                kb.copy("act", hT[:, :, i * 128:(i + 1) * 128],
                                        ptb[:, 0:1024].rearrange("p (c t) -> p c t", c=8), [pb], [hT_b])
                    S.barrier()

                    if 'B1' in PH:
                        with ExitStack() as st:
                            uT, uT_b = kb.sb(st, "uT", [128, L], BF16)
                            Bst, Bst_b = kb.sb(st, "BstL", [128, 32, 128], BF16)
                            Cst, Cst_b = kb.sb(st, "CstL", [128, 32, 128], BF16)
                            kb.load(Bst[:].rearrange("p g n -> p (g n)"), Bst_b, bst_d[l], bc_b[l])
                            kb.load(Cst[:].rearrange("p g n -> p (g n)"), Cst_b, cst_d[l], bc_b[l])
                            NLANE = int(os.environ.get('NLANE', '2'))
                            sbb = [[kb.sb(st, "sbb%d_%d" % (ln, i), [128, PAD + L], BF16) for i in range(2)] for ln in range(NLANE)]
                            qp = [[kb.sb(st, "qp%d_%d" % (ln, i), [128, 16, 128], BF16) for i in range(2)] for ln in range(NLANE)]
                            ytmp, ytmp_b = kb.sb(st, "ytmp", [128, 512], F32)
                            for ln in range(NLANE):
                                for i in range(2):
                                    kb.memset("pool", sbb[ln][i][0][:, 0:PAD], 0.0, [sbb[ln][i][1]])
                            evi = [0]

                            def evac(dst_ap, dstb, pt, pb):
                                evi[0] += 1
                                kb.copy("act" if evi[0] % 2 else "dve", dst_ap, pt[:], [pb], [dstb])
                            for c in range(4):
                                wv, wb_ = loadw("w_in", l, 8, c * 128, 128)
                                for b in range(4):
                                    pt, pb = nextps((0, 1, 2, 3))
                                    for kc in range(8):
                                        kb.mm(pt[:], wv[:, kc, :], hT[:, kc, b * 512:(b + 1) * 512], kc == 0, kc == 7,
                                              [wb_, hT_b], [pb])
                                    kb.copy("act", uT[:, b * 512:(b + 1) * 512], pt[:], [pb], [uT_b])
                                for g0 in range(0, 8, NLANE):
                                    lanes = []
                                    for ln in range(min(NLANE, 8 - g0)):
                                        gi = g0 + ln
                                        g = 8 * c + gi
                                        qa, qab = qp[ln][(g0 // NLANE) % 2]
                                        kb.load(qa[:], qab, qpow_d[l, g].rearrange("p (k n) -> p k n", k=16), qpow_b[l])
                                        lanes.append((gi, g, qa, qab, sbb[ln]))
                                    for gi, g, qa, qab, sb_ in lanes:
                                        for b in range(4):
                                            pt, pb = nextps((0, 1, 2, 3))
                                            kb.mm(pt[:], Bst[:, g, :], uT[:, b * 512:(b + 1) * 512], True, True, [Bst_b, uT_b], [pb])
                                            evac(sb_[0][0][:, PAD + b * 512:PAD + (b + 1) * 512], sb_[0][1], pt, pb)
                                    for k in range(6):
                                        base = 4 ** k
                                        for gi, g, qa, qab, sb_ in lanes:
                                            src, srcb = sb_[k % 2]
                                            dst, dstb = sb_[(k + 1) % 2]
                                            for b in range(4):
                                                mats = []
                                                for j in ((1, 2, 3) if k < 5 else (1,)):
                                                    m = j * base
                                                    if (b + 1) * 512 - m > 0:
                                                        mats.append((qa[:, 3 * k + j - 1, :], qab, PAD + b * 512 - m))
                                                dsl = slice(PAD + b * 512, PAD + (b + 1) * 512)
                                                if not mats:
                                                    kb.copy("pool", dst[:, dsl], src[:, dsl], [srcb], [dstb])
                                                    continue
                                                use_id = S5ID == 1 or (S5ID == 2 and b == 3) or (S5ID == 3 and b % 2 == 1)
                                                if use_id:
                                                    mats = [(ident_b, cbf_b, PAD + b * 512)] + mats
                                                pt, pb = nextps((0, 1, 2, 3))
                                                for mi, (lh, lhb, o0) in enumerate(mats):
                                                    kb.mm(pt[:], lh, src[:, o0:o0 + 512], mi == 0, mi == len(mats) - 1,
                                                          [lhb, srcb], [pb])
                                                if use_id:
                                                    kb.copy("act", dst[:, dsl], pt[:], [pb], [dstb])
                                                else:
                                                    kb.tt("dve", dst[:, dsl], pt[:], src[:, dsl], ALU.add, [pb, srcb], [dstb])
                                    for gi, g, qa, qab, sb_ in lanes:
                                        fin, finb = sb_[0]
                                        for b in range(4):
                                            yt, yb = psb[4 + b]
                                            kb.mm(yt[:], Cst[:, g, :], fin[:, PAD + b * 512:PAD + (b + 1) * 512], gi == 0, gi == 7,
                                                  [Cst_b, finb], [yb])
                                for b in range(4):
                                    yt, yb = psb[4 + b]
                                    kb.stt("dve", ytmp[:], uT[:, b * 512:(b + 1) * 512], dsk[:, c:c + 1], yt[:], ALU.mult, ALU.add,
                                           [uT_b, dsk_b, yb], [ytmp_b])
                                    kb.act(ysT[:, c, b * 512:(b + 1) * 512], ytmp[:], AF.Gelu_apprx_tanh, [ytmp_b], [ysT_b])
                    S.barrier()

                    if 'B2' in PH:
                        with ExitStack() as st:
                            qT = [kb.sb(st, "qT%d" % p, [128, L], BF16) for p in range(3)]
                            kAB = [kb.sb(st, "kAB%d" % i, [128, L], BF16) for i in range(2)]
                            Vp = [kb.sb(st, "Vp%d" % p, [128, 16, 128], BF16) for p in range(3)]
                            acc, acc_b = kb.sb(st, "acc", [128, 2, L], F32)
                            pT = [kb.sb(st, "pT%d" % i, [128, 256], BF16) for i in range(4)]
                            rden, rden_b = kb.sb(st, "rden", [128, L], F32)
                            vT, vT_b = kb.sb(st, "vT", [128, L], BF16)
                            kb.memset("pool", kAB[0][0][64:128, :], 0.0, [kAB[0][1]])
                            kb.memset("pool", kAB[1][0][0:64, :], 0.0, [kAB[1][1]])
                            pti = 0
                            for hp in range(4):
                                for p in range(3):
                                    wv, wb_ = loadw("w_in", l, 8, 512 + p * 512 + hp * 128, 128)
                                    for b in range(4):
                                        pt, pb = nextps()
                                        for kc in range(8):
                                            kb.mm(pt[:], wv[:, kc, :], hT[:, kc, b * 512:(b + 1) * 512], kc == 0, kc == 7,
                                                  [wb_, hT_b], [pb])
                                        kb.act(qT[p][0][:, b * 512:(b + 1) * 512], pt[:], AF.Copy, [pb], [qT[p][1]], scale=0.125)
                                wv, wb_ = loadw("w_in", l, 8, 2048 + hp * 128, 128)
                                for b in range(4):
                                    pt, pb = nextps()
                                    for kc in range(8):
                                        kb.mm(pt[:], wv[:, kc, :], hT[:, kc, b * 512:(b + 1) * 512], kc == 0, kc == 7,
                                              [wb_, hT_b], [pb])
                                    kb.copy("act", kAB[0][0][0:64, b * 512:(b + 1) * 512], pt[0:64, :], [pb], [kAB[0][1]])
                                    kb.copy("dve", kAB[1][0][64:128, b * 512:(b + 1) * 512], pt[64:128, :], [pb], [kAB[1][1]])
                                wv, wb_ = loadw("w_in", l, 8, 2560 + hp * 128, 128)
                                for b in range(4):
                                    pt, pb = nextps()
                                    for kc in range(8):
                                        kb.mm(pt[:], wv[:, kc, :], hT[:, kc, b * 512:(b + 1) * 512], kc == 0, kc == 7,
                                              [wb_, hT_b], [pb])
                                    kb.copy("dve", vT[:, b * 512:(b + 1) * 512], pt[:], [pb], [vT_b])
                                for p, (win, d) in enumerate(PATTERNS):
                                    nblk = L // d // 128
                                    blks = [(r, n) for r in range(d) for n in range(nblk)]
                                    for i0 in range(0, 16, 8):
                                        pt, pb = nextps()
                                        ptb = pt[:].bitcast(BF16)
                                        for j in range(8):
                                            r, n = blks[i0 + j]
                                            t0 = r + d * 128 * n
                                            kb.tr(ptb[:, j * 128:(j + 1) * 128], vT[:, t0:t0 + 127 * d + 1:d], ident_b, [vT_b, cbf_b], [pb])
                                        kb.copy("act", Vp[p][0][:, i0:i0 + 8, :],
                                                ptb[:, 0:1024].rearrange("p (j e) -> p j e", j=8), [pb], [Vp[p][1]])
                                pend = []

                                def flush(keep):
                                    while len(pend) > keep:
                                        pend.pop(0)()
                                for p, (win, d) in enumerate(PATTERNS):
                                    nblk = L // d // 128
                                    qt_, qtb = qT[p]
                                    vt_, vtb = Vp[p]
                                    for r in range(d):
                                        for n in range(nblk):
                                            t0 = r + d * 128 * n
                                            tq = slice(t0, t0 + 127 * d + 1, d)
                                            tprev = slice(t0 - 128 * d, t0 - d + 1, d)
                                            hasp = n >= 1
                                            wq = 256 if hasp else 128
                                            pnh = {}
                                            for hh in range(2):
                                                kt_, ktb = kAB[hh]
                                                ps_, psb_ = nextps()
                                                kb.mm(ps_[:, 0:128], kt_[:, tq], qt_[:, tq], True, False, [ktb, qtb], [psb_])
                                                kb.mm(ps_[:, 0:128], ident_b, mcur_b, False, True, [cbf_b], [psb_])
                                                if hasp:
                                                    kb.mm(ps_[:, 128:256], kt_[:, tprev], qt_[:, tq], True, False, [ktb, qtb], [psb_])
                                                    kb.mm(ps_[:, 128:256], ident_b, mprev_b, False, True, [cbf_b], [psb_])
                                                pe_, peb = pT[pti % 4]
                                                pti += 1
                                                kb.act(pe_[:, 0:wq], ps_[:, 0:wq], AF.Exp, [psb_], [peb])

                                                def pv(hh=hh, pe_=pe_, peb=peb, pnh=pnh, hasp=hasp, bid=r * nblk + n, tq=tq, p=p, vt_=vt_, vtb=vtb):
                                                    if hh == 0:
                                                        pnh["b"] = nextps()
                                                    pn, pnb = pnh["b"]
                                                    tp = (0, 64 * hh)
                                                    o_n = pn[64 * hh:64 * hh + 64, 0:128]
                                                    o_d = pn[64 * hh:64 * hh + 64, 128:256]
                                                    kb.mm(o_n, vt_[:, bid, 64 * hh:64 * hh + 64], pe_[:, 0:128], True, not hasp,
                                                          [vtb, peb], [pnb], tp=tp)
                                                    if hasp:
                                                        kb.mm(o_n, vt_[:, bid - 1, 64 * hh:64 * hh + 64], pe_[:, 128:256], False, True,
                                                              [vtb, peb], [pnb], tp=tp)
                                                    kb.mm(o_d, ones_b[:, 0:64], pe_[:, 0:128], True, not hasp, [cbf_b, peb], [pnb], tp=tp)
                                                    if hasp:
                                                        kb.mm(o_d, ones_b[:, 0:64], pe_[:, 128:256], False, True, [cbf_b, peb], [pnb], tp=tp)
                                                    if hh == 1:
                                                        src2 = pn[:, 0:256].rearrange("p (a t) -> p a t", a=2)
                                                        if p == 0:
                                                            kb.copy("dve", acc[:, :, tq], src2, [pnb], [acc_b])
                                                        else:
                                                            kb.tt("dve", acc[:, :, tq], acc[:, :, tq], src2, ALU.add, [acc_b, pnb], [acc_b])
                                                pend.append(pv)
                                                flush(2)
                                flush(0)
                                kb.recip(rden[:], acc[:, 1, :], [acc_b], [rden_b])
                                kb.tt("dve", oT[:, hp, :], acc[:, 0, :], rden[:], ALU.mult, [acc_b, rden_b], [oT_b])
                    S.barrier()

                    if 'B3' in PH:
                        with ExitStack() as st:
                            cwr, cwr_b = kb.sb(st, "cwr", [31, 512], F32)
                            kb.load(cwr[:], cwr_b, P["conv_w"][l])
                            cwT, cwT_b = kb.sb(st, "cwT", [128, 4, 31], F32)
                            cvp, cvp_b = kb.sb(st, "cvp", [128, 12], F32)
                            kb.load(cvp[:, 0:4], cvp_b, P["conv_b"][l].rearrange("(c p) -> p c", p=128), slow=True)
                            kb.load(cvp[:, 4:8], cvp_b, P["conv_ln_g"][l].rearrange("(c p) -> p c", p=128), slow=True)
                            kb.load(cvp[:, 8:12], cvp_b, P["conv_ln_b"][l].rearrange("(c p) -> p c", p=128), slow=True)
                            for c in range(4):
                                pt, pb = nextps()
                                kb.tr(pt[:, 0:31], cwr[:, c * 128:(c + 1) * 128], ident_f[0:31, 0:31], [cwr_b, cst_b], [pb])
                                kb.copy("dve", cwT[:, c, :], pt[:, 0:31], [pb], [cwT_b])
                            dg, dg_b = kb.sb(st, "dg", [128, 31, 128], BF16)
                            hpad, hpad_b = kb.sb(st, "hpad", [128, 30 + L], BF16)
                            sg_, sg_b = kb.sb(st, "sgl", [128, 512], F32)
                            cf, cf_b = kb.sb(st, "cf", [128, 4, L], F32)
                            sq2, sq2_b = kb.sb(st, "sq2", [128, 512], F32)
                            kb.memset("pool", hpad[:, 0:30], 0.0, [hpad_b])
                            for c in range(4):
                                for j in range(31):
                                    kb.ts("dve", dg[:, j, :], ident_f, cwT[:, c, j:j + 1], None, ALU.mult, None,
                                          [cst_b, cwT_b], [dg_b])
                                wa, wab = loadw("w_in", l, 8, 3072 + c * 128, 128)
                                wg, wgb = loadw("w_in", l, 8, 3584 + c * 128, 128)
                                for b in range(4):
                                    pa, pab = nextps()
                                    pg, pgb = nextps()
                                    for kc in range(8):
                                        kb.mm(pa[:], wa[:, kc, :], hT[:, kc, b * 512:(b + 1) * 512], kc == 0, kc == 7, [wab, hT_b], [pab])
                                    for kc in range(8):
                                        kb.mm(pg[:], wg[:, kc, :], hT[:, kc, b * 512:(b + 1) * 512], kc == 0, kc == 7, [wgb, hT_b], [pgb])
                                    kb.act(sg_[:], pg[:], AF.Sigmoid, [pgb], [sg_b])
                                    kb.tt("dve", hpad[:, 30 + b * 512:30 + (b + 1) * 512], pa[:], sg_[:], ALU.mult, [pab, sg_b], [hpad_b])
                                for b in range(4):
                                    pt, pb = nextps()
                                    for j in range(31):
                                        kb.mm(pt[:], dg[:, j, :], hpad[:, b * 512 + j:b * 512 + j + 512], j == 0, j == 30,
                                              [dg_b, hpad_b], [pb])
                                    kb.act(cf[:, c, b * 512:(b + 1) * 512], pt[:], AF.Identity, [pb, cvp_b], [cf_b], bias=cvp[:, c:c + 1])
                            mr, mr_b = kb.sb(st, "mr", [128, 2, 512], F32)
                            xc, xc_b = kb.sb(st, "xc", [128, 512], F32)
                            for b in range(4):
                                pm, pmb = nextps()
                                pv, pvb = nextps()
                                for c in range(4):
                                    kb.mm(pm[:], ones_f, cf[:, c, b * 512:(b + 1) * 512], c == 0, c == 3, [cst_b, cf_b], [pmb])
                                for c in range(4):
                                    kb.act(sq2[:], cf[:, c, b * 512:(b + 1) * 512], AF.Square, [cf_b], [sq2_b])
                                    kb.mm(pv[:], ones_f, sq2[:], c == 0, c == 3, [cst_b, sq2_b], [pvb])
                                kb.ts("dve", mr[:, 0, :], pm[:], 1.0 / 512, None, ALU.mult, None, [pmb], [mr_b])
                                kb.tt("dve", xc[:], mr[:, 0, :], mr[:, 0, :], ALU.mult, [mr_b], [xc_b])
                                kb.stt("dve", xc[:], pv[:], 1.0 / 512, xc[:], ALU.mult, ALU.subtract, [pvb, xc_b], [xc_b])
                                kb.ts("dve", xc[:], xc[:], 1e-6, None, ALU.add, None, [xc_b], [xc_b])
                                kb.act(xc[:], xc[:], AF.Sqrt, [xc_b], [xc_b])
                                kb.recip(mr[:, 1, :], xc[:], [xc_b], [mr_b])
                                for c in range(4):
                                    kb.tt("dve", xc[:], cf[:, c, b * 512:(b + 1) * 512], mr[:, 0, :], ALU.subtract, [cf_b, mr_b], [xc_b])
                                    kb.tt("dve", xc[:], xc[:], mr[:, 1, :], ALU.mult, [xc_b, mr_b], [xc_b])
                                    kb.act(hcT[:, c, b * 512:(b + 1) * 512], xc[:], AF.Silu, [xc_b, cvp_b], [hcT_b],
                                           bias=cvp[:, 8 + c:9 + c], scale=cvp[:, 4 + c:5 + c])
                    S.barrier()

                    if 'C' in PH:
                        with ExitStack() as st:
                            mg, mg_b = kb.sb(st, "mg", [128, 8, L], BF16)
                            bg, bg_b = kb.sb(st, "bg", [128, 24], F32)
                            kb.load(bg[:], bg_b, P["b_gate"][l].rearrange("(j p) -> p j", p=128), slow=True)
                            gs = [kb.sb(st, "gs%d" % i, [128, 512], F32) for i in range(3)]
                            ta, ta_b = kb.sb(st, "ta", [128, 512], F32)
                            tb2, tb2_b = kb.sb(st, "tb2", [128, 512], F32)
                            wcb = [kb.sb(st, "wcb%d" % i, [128, 1024], BF16) for i in range(7)]

                            def loadc(i, name, kc, c0):
                                t, b = wcb[i]
                                dst = t[:, 0:kc * 128].rearrange("p (k n) -> p k n", k=kc)
                                src = WB[name][l, :, c0:c0 + 128].rearrange("(k p) n -> p k n", p=128)
                                kb.load(dst, b, src, WBbuf[name][l])
                                return dst, b
                            for fc in range(8):
                                wg3 = [loadc(br, "w_in", 8, 4096 + br * 1024 + fc * 128) for br in range(3)]
                                wz1 = loadc(3, "w_ssm_glu", 4, fc * 128)
                                wz2 = loadc(4, "w_ssm_glu", 4, 1024 + fc * 128)
                                wau = loadc(5, "w_att_up", 4, fc * 128)
                                wpw = loadc(6, "w_conv_pw2", 4, fc * 128)
                                for b in range(4):
                                    bs = slice(b * 512, (b + 1) * 512)
                                    for br in range(3):
                                        pt, pb = nextps()
                                        for kc in range(8):
                                            kb.mm(pt[:], wg3[br][0][:, kc, :], hT[:, kc, bs], kc == 0, kc == 7, [wg3[br][1], hT_b], [pb])
                                        kb.act(gs[br][0][:], pt[:], AF.Sigmoid, [pb, bg_b], [gs[br][1]],
                                               bias=bg[:, br * 8 + fc:br * 8 + fc + 1])
                                    p1, p1b = nextps()
                                    p2, p2b = nextps()
                                    for kc in range(4):
                                        kb.mm(p1[:], wz1[0][:, kc, :], ysT[:, kc, bs], kc == 0, kc == 3, [wz1[1], ysT_b], [p1b])
                                    for kc in range(4):
                                        kb.mm(p2[:], wz2[0][:, kc, :], ysT[:, kc, bs], kc == 0, kc == 3, [wz2[1], ysT_b], [p2b])
                                    kb.act(ta[:], p2[:], AF.Sigmoid, [p2b], [ta_b])
                                    kb.tt("dve", ta[:], p1[:], ta[:], ALU.mult, [p1b, ta_b], [ta_b])
                                    kb.tt("dve", ta[:], ta[:], gs[0][0][:], ALU.mult, [ta_b, gs[0][1]], [ta_b])
                                    p3, p3b = nextps()
                                    for kc in range(4):
                                        kb.mm(p3[:], wau[0][:, kc, :], oT[:, kc, bs], kc == 0, kc == 3, [wau[1], oT_b], [p3b])
                                    kb.tt("dve", tb2[:], p3[:], gs[1][0][:], ALU.mult, [p3b, gs[1][1]], [tb2_b])
                                    kb.tt("pool", ta[:], ta[:], tb2[:], ALU.add, [ta_b, tb2_b], [ta_b])
                                    p4, p4b = nextps()
                                    for kc in range(4):
                                        kb.mm(p4[:], wpw[0][:, kc, :], hcT[:, kc, bs], kc == 0, kc == 3, [wpw[1], hcT_b], [p4b])
                                    kb.tt("dve", tb2[:], p4[:], gs[2][0][:], ALU.mult, [p4b, gs[2][1]], [tb2_b])
                                    kb.tt("dve", mg[:, fc, bs], ta[:], tb2[:], ALU.add, [ta_b, tb2_b], [mg_b])
                            wo = [loadw("w_out", l, 8, hf * 512, 512) for hf in range(2)]
                            xt = [kb.sb(st, "xc%d" % i, [128, D], F32) for i in range(2)]
                            for i in range(16):
                                xa, xab = xt[i % 2]
                                src, srcb = xsrc(i)
                                kb.load(xa[:], xab, src, srcb)
                                for hf in range(2):
                                    pt, pb = nextps()
                                    for kc in range(8):
                                        kb.mm(pt[:], mg[:, kc, i * 128:(i + 1) * 128], wo[hf][0][:, kc, :], kc == 0, kc == 7,
                                              [mg_b, wo[hf][1]], [pb])
                                    kb.tt("dve", xa[:, hf * 512:(hf + 1) * 512], xa[:, hf * 512:(hf + 1) * 512], pt[:], ALU.add,
                                          [xab, pb], [xab])
                                kb.store(xmid_d[i * 128:(i + 1) * 128, :], xmid_b[i], xa[:], xab)
                    S.barrier()

                if 'E' in PH:
                    with ExitStack() as st:
                        gbc, gbc_b = kb.sb(st, "g2bc", [128, D], F32)
                        kb.load(gbc[:], gbc_b, P["norm2_g"][l:l + 1, :].broadcast_to([128, D]))
                        if last:
                            gfc, gfc_b = kb.sb(st, "gfbc", [128, D], F32)
                            kb.load(gfc[:], gfc_b, final_g_d[0:1, :].broadcast_to([128, D]))
                        elif FUSEA:
                            gnx, gnx_b = kb.sb(st, "gnxbc", [128, D], F32)
                            kb.load(gnx[:], gnx_b, P["norm1_g"][l + 1:l + 2, :].broadcast_to([128, D]))
                            hbn = [kb.sb(st, "hbn%d" % i, [128, D], BF16) for i in range(2)]
                        xm2 = [[kb.sb(st, "xm%d_%d" % (q, i), [128, D], F32) for i in range(4)] for q in range(2)]
                        sqt, sqt_b = kb.sb(st, "sqtE", [128, D], F32)
                        hb = [kb.sb(st, "hbE%d" % i, [128, D], BF16) for i in range(4)]
                        sm = [kb.sb(st, "smE%d" % i, [128, 4], F32) for i in range(2)]
                        h2T2 = [kb.sb(st, "h2T%d" % q, [128, 8, 512], BF16) for q in range(2)]
                        gT, gT_b = kb.sb(st, "gT", [128, 22, 512], BF16)
                        wfo = [kb.sb(st, "wfo%d" % i, [128, 22, 512], BF16) for i in range(2)]
                        yo = [kb.sb(st, "yo%d" % i, [128, D], F32) for i in range(2)]
                        wfi = [0]

                        def norm_p1(b):
                            for j in range(4):
                                i = b * 4 + j
                                xa, xab = xm2[b % 2][j]
                                kb.load(xa[:], xab, xmid_d[i * 128:(i + 1) * 128, :], xmid_b[i])
                                kb.act(sqt[:], xa[:], AF.Square, [xab], [sqt_b])
                                s_, s_b = sm[i % 2]
                                kb.rsum(s_[:, 0:1], sqt[:], [sqt_b], [s_b])
                                kb.ts("dve", s_[:, 1:2], s_[:, 0:1], 1.0 / D, 1e-6, ALU.mult, ALU.add, [s_b], [s_b])
                                kb.act(s_[:, 2:3], s_[:, 1:2], AF.Sqrt, [s_b], [s_b])
                                kb.recip(s_[:, 3:4], s_[:, 2:3], [s_b], [s_b])
                                ha, hab = hb[j]
                                kb.stt("dve", ha[:], xa[:], s_[:, 3:4], gbc[:], ALU.mult, ALU.mult, [xab, s_b, gbc_b], [hab])

                        def norm_p2(b):
                            h2T, h2T_b = h2T2[b % 2]
                            for j in range(4):
                                ha, hab = hb[j]
                                pt, pb = nextps()
                                ptb = pt[:].bitcast(BF16)
                                for c in range(8):
                                    kb.tr(ptb[:, c * 128:(c + 1) * 128], ha[:, c * 128:(c + 1) * 128], ident_b, [hab, cbf_b], [pb])
                                kb.copy("act", h2T[:, :, j * 128:(j + 1) * 128],
                                        ptb[:, 0:1024].rearrange("p (c t) -> p c t", c=8), [pb], [h2T_b])

                        def ffn_in(b):
                            h2T, h2T_b = h2T2[b % 2]
                            for t in range(11):
                                wv, wb_ = loadw("w_ffn_in", l, 8, t * 512, 512)
                                for cq in range(4):
                                    cid = 4 * t + cq
                                    pt, pb = nextps()
                                    for kc in range(8):
                                        kb.mm(pt[:], wv[:, kc, cq * 128:(cq + 1) * 128], h2T[:, kc, :], kc == 0, kc == 7,
                                              [wb_, h2T_b], [pb])
                                    if cid < 22:
                                        kb.act(gT[:, cid, :], pt[:], AF.Silu, [pb], [gT_b])
                                    else:
                                        kb.tt("dve", gT[:, cid - 22, :], gT[:, cid - 22, :], pt[:], ALU.mult, [gT_b, pb], [gT_b])

                        def ffn_out_q(b, qd):
                            if qd % 2:
                                return
                            hf = qd // 2
                            wt, wtb = wfo[wfi[0] % 2]
                            wfi[0] += 1
                            kb.load(wt[:], wtb, WB["w_ffn_out"][l, :, hf * 512:(hf + 1) * 512].rearrange("(k p) n -> p k n", p=128),
                                    WBbuf["w_ffn_out"][l])
                            for j in range(4):
                                xa, xab = xm2[b % 2][j]
                                pt, pb = nextps()
                                for kc in range(22):
                                    kb.mm(pt[:], gT[:, kc, j * 128:(j + 1) * 128], wt[:, kc, :], kc == 0, kc == 21,
                                          [gT_b, wtb], [pb])
                                kb.tt("dve", xa[:, hf * 512:(hf + 1) * 512], xa[:, hf * 512:(hf + 1) * 512], pt[:], ALU.add,
                                      [xab, pb], [xab])
                        norm_p1(0)
                        norm_p2(0)
                        for b in range(4):
                            xm = xm2[b % 2]
                            ffn_in(b)
                            if b + 1 < 4:
                                norm_p1(b + 1)
                            ffn_out_q(b, 0)
                            ffn_out_q(b, 1)
                            if b + 1 < 4:
                                norm_p2(b + 1)
                            ffn_out_q(b, 2)
                            ffn_out_q(b, 3)
                            for j in range(4):
                                i = b * 4 + j
                                xa, xab = xm[j]
                                if not last:
                                    kb.store(xout_d[i * 128:(i + 1) * 128, :], xout_b[i], xa[:], xab)
                                    if FUSEA:
                                        kb.act(sqt[:], xa[:], AF.Square, [xab], [sqt_b])
                                        s_, s_b = sm[i % 2]
                                        kb.rsum(s_[:, 0:1], sqt[:], [sqt_b], [s_b])
                                        kb.ts("dve", s_[:, 1:2], s_[:, 0:1], 1.0 / D, 1e-6, ALU.mult, ALU.add, [s_b], [s_b])
                                        kb.act(s_[:, 2:3], s_[:, 1:2], AF.Sqrt, [s_b], [s_b])
                                        kb.recip(s_[:, 3:4], s_[:, 2:3], [s_b], [s_b])
                                        ha, hab = hbn[i % 2]
                                        kb.stt("dve", ha[:], xa[:], s_[:, 3:4], gnx[:], ALU.mult, ALU.mult, [xab, s_b, gnx_b], [hab])
                                        pt, pb = nextps()
                                        ptb = pt[:].bitcast(BF16)
                                        for c in range(8):
                                            kb.tr(ptb[:, c * 128:(c + 1) * 128], ha[:, c * 128:(c + 1) * 128], ident_b, [hab, cbf_b], [pb])
                                        kb.copy("act", hT[:, :, i * 128:(i + 1) * 128],
                                                ptb[:, 0:1024].rearrange("p (c t) -> p c t", c=8), [pb], [hT_b])
                                else:
                                    kb.act(sqt[:], xa[:], AF.Square, [xab], [sqt_b])
                                    s_, s_b = sm[i % 2]
                                    kb.rsum(s_[:, 0:1], sqt[:], [sqt_b], [s_b])
                                    kb.ts("dve", s_[:, 1:2], s_[:, 0:1], 1.0 / D, 1e-6, ALU.mult, ALU.add, [s_b], [s_b])
                                    kb.act(s_[:, 2:3], s_[:, 1:2], AF.Sqrt, [s_b], [s_b])
                                    kb.recip(s_[:, 3:4], s_[:, 2:3], [s_b], [s_b])
                                    ya, yab = yo[i % 2]
                                    kb.stt("dve", ya[:], xa[:], s_[:, 3:4], gfc[:], ALU.mult, ALU.mult, [xab, s_b, gfc_b], [yab])
                                    kb.store(out_d[sq * L + i * 128: sq * L + (i + 1) * 128, :], out_b, ya[:], yab)
                S.barrier()
        S.barrier(full=True)
        S.run(st0)
    return nc


def make_consts():
    c = np.zeros((128, NCONST), np.float32)
    i = np.arange(128)
    c[:, 0:128] = np.eye(128)
    j = np.zeros((128, 128), np.float32)
    j[i[:64], i[:64] + 64] = 1.0
    j[i[:64] + 64, i[:64]] = 1.0
    c[:, 128:256] = j
    k = i[:, None]
    q = i[None, :]
    c[:, 256:384] = np.where(k <= q, 0.0, NEG)
    c[:, 384:512] = np.where(k >= q, 0.0, NEG)
    c[:, 512:640] = 1.0
    c[:, 640:648] = (i[:, None] // 16 == np.arange(8)[None, :]).astype(np.float32)
    c[:64, 648] = -1.0
    c[64:, 648] = 1.0
    return c


_NC_CACHE = {}


def run(inputs, NSEQ, DEPTH, ncores=8):
    key = (NSEQ, DEPTH)
    if key not in _NC_CACHE:
        _NC_CACHE[key] = build(NSEQ, DEPTH)
    nc = _NC_CACHE[key]
    consts = make_consts()
    x = np.asarray(inputs["x"], np.float32)
    in_maps = []
    for c in range(ncores):
        m = {"x": np.ascontiguousarray(x[c * NSEQ:(c + 1) * NSEQ].reshape(NSEQ * L, D)),
             "consts": consts,
             "final_g": np.ascontiguousarray(np.asarray(inputs["final_g"], np.float32).reshape(1, D))}
        for n, shp in PNAMES:
            m[n] = np.ascontiguousarray(np.asarray(inputs[n], np.float32)[:DEPTH])
        in_maps.append(m)
    res = run_bass_kernel_spmd(nc, in_maps, core_ids=list(range(ncores)))
    outs = [r["out"].reshape(NSEQ, L, D) for r in res.results]
    return np.concatenate(outs, axis=0).astype(np.float32)


def kernel(**inputs):
    return run(inputs, 4, 2, 8)
```

```python
import math
import os
from contextlib import ExitStack
import numpy as np
import concourse.bass as bass
import concourse.mybir as mybir
from concourse.bass_utils import run_bass_kernel_spmd

F32 = mybir.dt.float32
BF16 = mybir.dt.bfloat16
AF = mybir.ActivationFunctionType
ALU = mybir.AluOpType
AX = mybir.AxisListType

L = 2048
D = 1024
DFF = 2816
NEG = -30000.0
NCONST = 128 * 5 + 8 + 1
PAD = 1024


class Buf:
    __slots__ = ("lw", "rd", "ds", "dso", "excl")

    def __init__(self, excl=False):
        self.excl = excl
        self.lw = None
        self.rd = {}
        self.ds = None
        self.dso = None


class Sched:
    ENG = ("pe", "act", "dve", "pool", "sp")

    def __init__(self, nc):
        self.nc = nc
        self.ops = {e: [] for e in self.ENG}
        self.cnt = {e: 0 for e in self.ENG}
        self.waited = {e: {} for e in self.ENG}
        self.dcnt = {}
        self.nobar = set()

    def _deps(self, reads, writes):
        deps = {}
        for b in reads:
            if b.lw is not None and deps.get(b.lw[0], 0) < b.lw[1]:
                deps[b.lw[0]] = b.lw[1]
            if b.excl:
                for k, v in b.rd.items():
                    if deps.get(k, 0) < v:
                        deps[k] = v
        for b in writes:
            if b.lw is not None and deps.get(b.lw[0], 0) < b.lw[1]:
                deps[b.lw[0]] = b.lw[1]
            for k, v in b.rd.items():
                if deps.get(k, 0) < v:
                    deps[k] = v
        return deps

    def _waits(self, eng, deps, skip_self):
        w = []
        wd = self.waited[eng]
        for k, v in deps.items():
            if skip_self and k == eng:
                continue
            if wd.get(k, 0) >= v:
                continue
            wd[k] = v
            w.append((k, v))
        return w

    def _fin(self, ev, reads, writes):
        k, v = ev
        for b in reads:
            if b.rd.get(k, 0) < v:
                b.rd[k] = v
        for b in writes:
            b.lw = ev
            b.rd = {}

    def op(self, eng, fn, reads=(), writes=()):
        w = self._waits(eng, self._deps(reads, writes), eng == "pe")
        self.cnt[eng] += 1
        ev = (eng, self.cnt[eng])
        self.ops[eng].append((w, fn, (eng, 1)))
        self._fin(ev, reads, writes)

    def dma(self, q, dsem, fn, reads=(), writes=()):
        w = self._waits(q, self._deps(reads, writes), False)
        key = ("d", dsem)
        self.dcnt[key] = self.dcnt.get(key, 0) + 16
        ev = (key, self.dcnt[key])
        self.ops[q].append((w, fn, (key, 16)))
        self._fin(ev, reads, writes)

    def barrier(self, full=False):
        for e in self.ENG:
            deps = {k: v for k, v in self.cnt.items() if v > 0}
            deps.update({k: v for k, v in self.dcnt.items() if full or k not in self.nobar})
            w = self._waits(e, deps, True)
            if w:
                self.ops[e].append((w, None, None))

    def run(self, stack):
        nc = self.nc
        sems = {}
        for e in self.ENG:
            sems[e] = stack.enter_context(nc.semaphore("s_" + e))
        for key in self.dcnt:
            sems[key] = stack.enter_context(nc.semaphore("d%d" % key[1]))
        block = stack.enter_context(nc.Block())
        ops = self.ops

        def replay(e, eng):
            for w, fn, inc in ops[e]:
                for k, v in w:
                    eng.wait_ge(sems[k], v)
                if fn is not None:
                    fn(eng).then_inc(sems[inc[0]], inc[1])

        @block.tensor
        def _(eng):
            replay("pe", eng)

        @block.scalar
        def _(eng):
            replay("act", eng)

        @block.vector
        def _(eng):
            replay("dve", eng)

        @block.gpsimd
        def _(eng):
            replay("pool", eng)

        @block.sync
        def _(eng):
            replay("sp", eng)


class KB:
    def __init__(self, nc):
        self.nc = nc
        self.S = Sched(nc)
        self.nds = 0

    def sb(self, st, name, shape, dt):
        self.nsb = getattr(self, "nsb", 0) + 1
        if not hasattr(self, "bufs"):
            self.bufs = {}
        b = self.bufs.get(name)
        if b is None:
            b = self.bufs[name] = Buf()
        return st.enter_context(self.nc.sbuf_tensor("%s_u%d" % (name, self.nsb), shape, dt)), b

    def load(self, dst, dbuf, src, sbuf=None, q="sp", slow=False):
        if dbuf.ds is None:
            dbuf.ds = self.nds
            self.nds += 1
        kw = {"allow_slow_non_contiguous": True} if slow else {}
        self.S.dma(q, dbuf.ds, lambda e: e.dma_start(out=dst, in_=src, **kw),
                   reads=[sbuf] if sbuf is not None else [], writes=[dbuf])

    def store(self, dst, dbuf, src, sbuf, q="sp"):
        if sbuf.dso is None:
            sbuf.dso = self.nds
            self.nds += 1
        self.S.dma(q, sbuf.dso, lambda e: e.dma_start(out=dst, in_=src), reads=[sbuf], writes=[dbuf])

    def mm(self, out, lhsT, rhs, start, stop, reads, writes, tp=None):
        kw = {} if tp is None else {"tile_position": tp}
        self.S.op("pe", lambda e: e.matmul(out, lhsT=lhsT, rhs=rhs, start=start, stop=stop, **kw), reads, writes)

    def tr(self, out, in_, ident, reads, writes):
        self.S.op("pe", lambda e: e.transpose(out=out, in_=in_, identity=ident), reads, writes)

    def act(self, out, in_, func, reads, writes, bias=None, scale=None):
        kw = {}
        if bias is not None:
            kw["bias"] = bias
        if scale is not None:
            kw["scale"] = scale
        self.S.op("act", lambda e: e.activation(out=out, in_=in_, func=func, **kw), reads, writes)

    def tt(self, eng, out, in0, in1, op, reads, writes):
        self.S.op(eng, lambda e: e.tensor_tensor(out=out, in0=in0, in1=in1, op=op), reads, writes)

    def ts(self, eng, out, in0, s1, s2, op0, op1, reads, writes):
        if s2 is None:
            self.S.op(eng, lambda e: e.tensor_scalar(out=out, in0=in0, scalar1=s1, scalar2=None, op0=op0), reads, writes)
        else:
            self.S.op(eng, lambda e: e.tensor_scalar(out=out, in0=in0, scalar1=s1, scalar2=s2, op0=op0, op1=op1),
                      reads, writes)

    def stt(self, eng, out, in0, scalar, in1, op0, op1, reads, writes):
        self.S.op(eng, lambda e: e.scalar_tensor_tensor(out=out, in0=in0, scalar=scalar, in1=in1, op0=op0, op1=op1),
                  reads, writes)

    def copy(self, eng, out, in_, reads, writes):
        if eng == "act":
            self.act(out, in_, AF.Copy, reads, writes)
        else:
            self.S.op(eng, lambda e: e.tensor_copy(out=out, in_=in_), reads, writes)

    def memset(self, eng, ap, val, writes):
        self.S.op(eng, lambda e: e.memset(ap, val), (), writes)

    def recip(self, out, in_, reads, writes):
        self.S.op("dve", lambda e: e.reciprocal(out=out, in_=in_), reads, writes)

    def rsum(self, out, in_, reads, writes):
        self.S.op("dve", lambda e: e.reduce_sum(out=out, in_=in_, axis=AX.X), reads, writes)


PNAMES = [
    ("norm1_g", (D,)), ("w_in", (D, 7168)), ("b_gate", (3072,)), ("ssm_lambda_re", (32, 64)),
    ("ssm_lambda_im", (32, 64)), ("ssm_log_dt", (32,)), ("ssm_b_re", (32, 64, 16)), ("ssm_b_im", (32, 64, 16)),
    ("ssm_c_re", (32, 16, 64)), ("ssm_c_im", (32, 16, 64)), ("ssm_d", (512,)), ("w_ssm_glu", (512, 2048)),
    ("w_att_up", (512, D)), ("conv_w", (31, 512)), ("conv_b", (512,)), ("conv_ln_g", (512,)),
    ("conv_ln_b", (512,)), ("w_conv_pw2", (512, D)), ("w_out", (D, D)), ("norm2_g", (D,)),
    ("w_ffn_in", (D, 2 * DFF)), ("w_ffn_out", (DFF, D)),
]
BIGW = ["w_in", "w_ssm_glu", "w_att_up", "w_conv_pw2", "w_out", "w_ffn_in", "w_ffn_out"]
PATTERNS = ((128, 1), (512, 4), (2048, 16))
PSTOP = int(os.environ.get('PSTOP', '9'))
B1STOP = int(os.environ.get('B1STOP', '9'))
B1VAR = os.environ.get('B1VAR', '')
S5ID = int(os.environ.get('S5ID', '3'))
FUSEA = int(os.environ.get('FUSEA', '0'))
PH = set(os.environ.get('KPH', 'prep,A,B1,B2,B3,C,E').split(','))


def build(NSEQ, DEPTH):
    nc = bass.Bass("TRN2", target_bir_lowering=False)
    kb = KB(nc)
    S = kb.S
    x_d = nc.dram_tensor("x", [NSEQ * L, D], F32, kind="ExternalInput").ap()
    out_d = nc.dram_tensor("out", [NSEQ * L, D], F32, kind="ExternalOutput").ap()
    consts_d = nc.dram_tensor("consts", [128, NCONST], F32, kind="ExternalInput").ap()
    final_g_d = nc.dram_tensor("final_g", [1, D], F32, kind="ExternalInput").ap()
    P = {}
    for n, shp in PNAMES:
        P[n] = nc.dram_tensor(n, [DEPTH] + list(shp), F32, kind="ExternalInput").ap()
    WB = {}
    WBbuf = {}
    for n in BIGW:
        shp = dict(PNAMES)[n]
        WB[n] = nc.dram_tensor(n + "_bf", [DEPTH] + list(shp), BF16, kind="Internal").ap()
        WBbuf[n] = [Buf() for _ in range(DEPTH)]
    qpow_d = nc.dram_tensor("qpow", [DEPTH, 32, 128, 16 * 128], BF16, kind="Internal").ap()
    qpow_b = [Buf() for _ in range(DEPTH)]
    bst_d = nc.dram_tensor("bst_d", [DEPTH, 128, 4096], BF16, kind="Internal").ap()
    cst_d = nc.dram_tensor("cst_d", [DEPTH, 128, 4096], BF16, kind="Internal").ap()
    bc_b = [Buf() for _ in range(DEPTH)]
    xmid_d = nc.dram_tensor("xmid", [L, D], F32, kind="Internal").ap()
    xout_d = nc.dram_tensor("xout", [L, D], F32, kind="Internal").ap()
    xmid_b = [Buf() for _ in range(16)]
    xout_b = [Buf() for _ in range(16)]
    out_b = Buf()

    with ExitStack() as st0:
        for l in range(DEPTH):
            for n in BIGW:
                K = dict(PNAMES)[n][0]
                for r in range(0, K, 128):
                    kb.load(WB[n][l, r:r + 128, :], WBbuf[n][l], P[n][l, r:r + 128, :], q="pool")
                S.nobar.add(("d", WBbuf[n][l].ds))

        cst, cst_b = kb.sb(st0, "cst", [128, NCONST], F32)
        kb.load(cst[:], cst_b, consts_d)
        ident_f = cst[:, 0:128]
        jswap_f = cst[:, 128:256]
        ones_f = cst[:, 512:640]
        rowmask = cst[:, 640:648]
        sgn = cst[:, 648:649]
        cbf, cbf_b = kb.sb(st0, "cbf", [128, 640], BF16)
        kb.copy("dve", cbf[:], cst[:, 0:640], [cst_b], [cbf_b])
        ident_b = cbf[:, 0:128]
        mcur_b = cbf[:, 256:384]
        mprev_b = cbf[:, 384:512]
        ones_b = cbf[:, 512:640]

        psb = []
        for i in range(8):
            t = st0.enter_context(nc.psum_tensor("ps%d" % i, [128, 512], F32))
            psb.append((t, Buf(excl=True)))
        psi = [0]

        def nextps(pool=(0, 1, 2, 3, 4, 5, 6, 7)):
            psi[0] += 1
            return psb[pool[psi[0] % len(pool)]]

        NSLOT = 4
        ring = [kb.sb(st0, "wr%d" % i, [128, 4096], BF16) for i in range(NSLOT)]
        ri = [0]

        def loadw(name, l, kc, c0, ncols):
            t, b = ring[ri[0] % NSLOT]
            ri[0] += 1
            dst = t[:, 0:kc * ncols].rearrange("p (k n) -> p k n", k=kc)
            src = WB[name][l, :, c0:c0 + ncols].rearrange("(k p) n -> p k n", p=128)
            kb.load(dst, b, src, WBbuf[name][l])
            return dst, b

        s5c = []
        for l in (range(DEPTH) if 'prep' in PH else ()):
            dsk, dsk_b = kb.sb(st0, "dsk%d" % l, [128, 4], F32)
            s5c.append((dsk, dsk_b))
            with ExitStack() as st:
                Bst, Bst_b = kb.sb(st, "Bst%d" % l, [128, 32, 128], BF16)
                Cst, Cst_b = kb.sb(st, "Cst%d" % l, [128, 32, 128], BF16)
                def T(name, w, dt=F32):
                    return kb.sb(st, "%s_%d" % (name, l), [128, w], dt)
                lr, lr_b = T("lr", 32)
                li, li_b = T("li", 32)
                ldt, ldt_b = T("ldt", 32)
                for h in (0, 64):
                    kb.load(lr[h:h + 64, :], lr_b, P["ssm_lambda_re"][l].rearrange("g p -> p g"), slow=True)
                    kb.load(li[h:h + 64, :], li_b, P["ssm_lambda_im"][l].rearrange("g p -> p g"), slow=True)
                kb.load(ldt[:], ldt_b, P["ssm_log_dt"][l:l + 1, :].broadcast_to([128, 32]))
                X1, X1_b = T("X1", 512)
                X2, X2_b = T("X2", 512)
                bre = P["ssm_b_re"][l].rearrange("g p h -> p g h")
                bim = P["ssm_b_im"][l].rearrange("g p h -> p g h")
                X1v = X1[:].rearrange("p (g h) -> p g h", h=16)
                X2v = X2[:].rearrange("p (g h) -> p g h", h=16)
                kb.load(X1v[0:64], X1_b, bre)
                kb.load(X1v[64:128], X1_b, bim)
                kb.load(X2v[0:64], X2_b, bim)
                kb.load(X2v[64:128], X2_b, bre)
                CT, CT_b = T("CT", 512)
                for c in range(4):
                    kb.load(CT[:, c * 128:c * 128 + 64], CT_b,
                            P["ssm_c_re"][l, 8 * c:8 * c + 8].rearrange("g h p -> (g h) p"))
                    kb.load(CT[:, c * 128 + 64:c * 128 + 128], CT_b,
                            P["ssm_c_im"][l, 8 * c:8 * c + 8].rearrange("g h p -> (g h) p"))
                kb.load(dsk[:], dsk_b, P["ssm_d"][l].rearrange("(c p) -> p c", p=128), slow=True)
                dt_, dt_b = T("dt", 32)
                kb.act(dt_[:], ldt[:], AF.Exp, [ldt_b], [dt_b])
                t1, t1_b = T("t1", 32)
                t2, t2_b = T("t2", 32)
                t3, t3_b = T("t3", 32)
                mag, mag_b = T("mag", 32)
                cc, cc_b = T("cc", 32)
                ss, ss_b = T("ss", 32)
                kb.tt("dve", t1[:], lr[:], dt_[:], ALU.mult, [lr_b, dt_b], [t1_b])
                kb.act(mag[:], t1[:], AF.Exp, [t1_b], [mag_b])
                kb.tt("dve", t2[:], li[:], dt_[:], ALU.mult, [li_b, dt_b], [t2_b])
                kb.ts("dve", t2[:], t2[:], 1.0 / 16.0, None, ALU.mult, None, [t2_b], [t2_b])
                kb.act(ss[:], t2[:], AF.Sin, [t2_b], [ss_b])
                kb.ts("dve", t3[:], t2[:], math.pi / 2, None, ALU.add, None, [t2_b], [t3_b])
                kb.act(cc[:], t3[:], AF.Sin, [t3_b], [cc_b])
                for _ in range(4):
                    kb.tt("dve", t1[:], cc[:], cc[:], ALU.mult, [cc_b], [t1_b])
                    kb.tt("dve", t2[:], ss[:], ss[:], ALU.mult, [ss_b], [t2_b])
                    kb.tt("dve", t3[:], cc[:], ss[:], ALU.mult, [cc_b, ss_b], [t3_b])
                    kb.tt("dve", cc[:], t1[:], t2[:], ALU.subtract, [t1_b, t2_b], [cc_b])
                    kb.ts("dve", ss[:], t3[:], 2.0, None, ALU.mult, None, [t3_b], [ss_b])
                ar, ar_b = T("ar", 32)
                ai, ai_b = T("ai", 32)
                kb.tt("dve", ar[:], mag[:], cc[:], ALU.mult, [mag_b, cc_b], [ar_b])
                kb.tt("dve", ai[:], mag[:], ss[:], ALU.mult, [mag_b, ss_b], [ai_b])
                nr, nr_b = T("nr", 32)
                kb.ts("dve", nr[:], ar[:], -1.0, None, ALU.add, None, [ar_b], [nr_b])
                den, den_b = T("den", 32)
                kb.tt("dve", t1[:], lr[:], lr[:], ALU.mult, [lr_b], [t1_b])
                kb.tt("dve", t2[:], li[:], li[:], ALU.mult, [li_b], [t2_b])
                kb.tt("dve", den[:], t1[:], t2[:], ALU.add, [t1_b, t2_b], [den_b])
                kb.recip(den[:], den[:], [den_b], [den_b])
                zr, zr_b = T("zr", 32)
                zi, zi_b = T("zi", 32)
                kb.tt("dve", t1[:], nr[:], lr[:], ALU.mult, [nr_b, lr_b], [t1_b])
                kb.tt("dve", t2[:], ai[:], li[:], ALU.mult, [ai_b, li_b], [t2_b])
                kb.tt("dve", t3[:], t1[:], t2[:], ALU.add, [t1_b, t2_b], [t3_b])
                kb.tt("dve", zr[:], t3[:], den[:], ALU.mult, [t3_b, den_b], [zr_b])
                kb.tt("dve", t1[:], ai[:], lr[:], ALU.mult, [ai_b, lr_b], [t1_b])
                kb.tt("dve", t2[:], nr[:], li[:], ALU.mult, [nr_b, li_b], [t2_b])
                kb.tt("dve", t3[:], t1[:], t2[:], ALU.subtract, [t1_b, t2_b], [t3_b])
                kb.tt("dve", zi[:], t3[:], den[:], ALU.mult, [t3_b, den_b], [zi_b])
                kb.ts("dve", zi[:], zi[:], sgn, None, ALU.mult, None, [zi_b, cst_b], [zi_b])
                zrb = zr[:].unsqueeze(2).broadcast_to([128, 32, 16])
                zib = zi[:].unsqueeze(2).broadcast_to([128, 32, 16])
                kb.tt("dve", X1v, X1v, zrb, ALU.mult, [X1_b, zr_b], [X1_b])
                kb.tt("dve", X2v, X2v, zib, ALU.mult, [X2_b, zi_b], [X2_b])
                kb.tt("dve", X1[:], X1[:], X2[:], ALU.add, [X1_b, X2_b], [X1_b])
                kb.memset("dve", Cst[:], 0.0, [Cst_b])
                for c in (range(4) if PSTOP >= 2 else ()):
                    pt, pb = nextps()
                    kb.tr(pt[:, 0:128], X1[:, c * 128:(c + 1) * 128], ident_f, [X1_b, cst_b], [pb])
                    for gi in range(8):
                        kb.ts("dve", Bst[:, 8 * c + gi, :], pt[:, 0:128], rowmask[:, gi:gi + 1], None, ALU.mult, None,
                              [pb, cst_b], [Bst_b])
                    kb.ts("dve", CT[:, c * 128 + 64:c * 128 + 128], CT[:, c * 128 + 64:c * 128 + 128], -1.0, None,
                          ALU.mult, None, [CT_b], [CT_b])
                    pt2, pb2 = nextps()
                    kb.tr(pt2[:, 0:128], CT[:, c * 128:(c + 1) * 128], ident_f, [CT_b, cst_b], [pb2])
                    for gi in range(8):
                        kb.copy("act", Cst[:, 8 * c + gi, gi * 16:(gi + 1) * 16], pt2[:, gi * 16:(gi + 1) * 16],
                                [pb2], [Cst_b])
                NQ = 16
                arK, arK_b = T("arK", NQ * 32)
                aiK, aiK_b = T("aiK", NQ * 32)
                c2K, c2K_b = T("c2K", NQ * 32)

                def qs(q):
                    return slice(q * 32, (q + 1) * 32)

                def cmul(qo, qa_, qb_):
                    xr, xi = arK[:, qs(qa_)], aiK[:, qs(qa_)]
                    yr, yi = arK[:, qs(qb_)], aiK[:, qs(qb_)]
                    kb.tt("dve", t1[:], xr, yr, ALU.mult, [arK_b], [t1_b])
                    kb.tt("dve", t2[:], xi, yi, ALU.mult, [aiK_b], [t2_b])
                    kb.tt("dve", arK[:, qs(qo)], t1[:], t2[:], ALU.subtract, [t1_b, t2_b], [arK_b])
                    kb.tt("dve", t1[:], xr, yi, ALU.mult, [arK_b, aiK_b], [t1_b])
                    kb.tt("dve", t2[:], xi, yr, ALU.mult, [arK_b, aiK_b], [t2_b])
                    kb.tt("dve", aiK[:, qs(qo)], t1[:], t2[:], ALU.add, [t1_b, t2_b], [aiK_b])
                kb.copy("dve", arK[:, 0:32], ar[:], [ar_b], [arK_b])
                kb.copy("dve", aiK[:, 0:32], ai[:], [ai_b], [aiK_b])
                for k in range(5):
                    cmul(3 * k + 1, 3 * k, 3 * k)
                    cmul(3 * k + 2, 3 * k + 1, 3 * k)
                    cmul(3 * k + 3, 3 * k + 1, 3 * k + 1)
                kb.ts("dve", c2K[:], aiK[:], sgn, -1.0, ALU.mult, ALU.mult, [aiK_b, cst_b], [c2K_b])
                stg = [kb.sb(st, "stg%d_%d" % (i, l), [128, NQ * 128], BF16) for i in range(2)]
                stmp = [kb.sb(st, "stmp%d_%d" % (i, l), [128, NQ * 128], BF16) for i in range(2)]
                idb = ident_f.unsqueeze(1).broadcast_to([128, NQ, 128])
                jsb = jswap_f.unsqueeze(1).broadcast_to([128, NQ, 128])
                for g in (range(32) if PSTOP >= 3 else ()):
                    sg, sg_b = stg[g % 2]
                    tm, tm_b = stmp[g % 2]
                    sgv = sg[:].rearrange("p (k n) -> p k n", k=NQ)
                    tmv = tm[:].rearrange("p (k n) -> p k n", k=NQ)
                    arv = arK[:].rearrange("p (k g) -> p k g", g=32)[:, :, g:g + 1].broadcast_to([128, NQ, 128])
                    c2v = c2K[:].rearrange("p (k g) -> p k g", g=32)[:, :, g:g + 1].broadcast_to([128, NQ, 128])
                    kb.tt("dve", tmv, jsb, c2v, ALU.mult, [cst_b, c2K_b], [tm_b])
                    kb.tt("dve", sgv, idb, arv, ALU.mult, [cst_b, arK_b], [sg_b])
                    kb.tt("dve", sg[:], sg[:], tm[:], ALU.add, [sg_b, tm_b], [sg_b])
                    kb.store(qpow_d[l, g], qpow_b[l], sg[:], sg_b)
                if PSTOP >= 4:
                    kb.store(bst_d[l], bc_b[l], Bst[:].rearrange("p g n -> p (g n)"), Bst_b)
                    kb.store(cst_d[l], bc_b[l], Cst[:].rearrange("p g n -> p (g n)"), Cst_b)
            S.barrier()

        if FUSEA:
            hT, hT_b = kb.sb(st0, "hT", [128, 8, L], BF16)
        for sq in range(NSEQ):
            for l in range(DEPTH):
                last = (l == DEPTH - 1)
                dsk, dsk_b = s5c[l] if s5c else (None, None)

                def xsrc(i):
                    if l == 0:
                        return x_d[sq * L + i * 128: sq * L + (i + 1) * 128, :], None
                    return xout_d[i * 128:(i + 1) * 128, :], xout_b[i]

                with ExitStack() as stB:
                    if not FUSEA:
                        hT, hT_b = kb.sb(stB, "hT", [128, 8, L], BF16)
                    ysT, ysT_b = kb.sb(stB, "ysT", [128, 4, L], BF16)
                    oT, oT_b = kb.sb(stB, "oT", [128, 4, L], BF16)
                    hcT, hcT_b = kb.sb(stB, "hcT", [128, 4, L], BF16)
                    if 'A' in PH and (l == 0 or not FUSEA):
                        with ExitStack() as st:
                            gbc, gbc_b = kb.sb(st, "gbc", [128, D], F32)
                            kb.load(gbc[:], gbc_b, P["norm1_g"][l:l + 1, :].broadcast_to([128, D]))
                            xt = [kb.sb(st, "xt%d" % i, [128, D], F32) for i in range(2)]
                            sqt, sqt_b = kb.sb(st, "sqt", [128, D], F32)
                            hb = [kb.sb(st, "hb%d" % i, [128, D], BF16) for i in range(2)]
                            sm = [kb.sb(st, "sm%d" % i, [128, 4], F32) for i in range(2)]
                            for i in range(16):
                                xa, xb_ = xt[i % 2]
                                src, srcb = xsrc(i)
                                kb.load(xa[:], xb_, src, srcb)
                                kb.act(sqt[:], xa[:], AF.Square, [xb_], [sqt_b])
                                s_, s_b = sm[i % 2]
                                kb.rsum(s_[:, 0:1], sqt[:], [sqt_b], [s_b])
                                kb.ts("dve", s_[:, 1:2], s_[:, 0:1], 1.0 / D, 1e-6, ALU.mult, ALU.add, [s_b], [s_b])
                                kb.act(s_[:, 2:3], s_[:, 1:2], AF.Sqrt, [s_b], [s_b])
                                kb.recip(s_[:, 3:4], s_[:, 2:3], [s_b], [s_b])
                                ha, hab = hb[i % 2]
                                kb.stt("dve", ha[:], xa[:], s_[:, 3:4], gbc[:], ALU.mult, ALU.mult, [xb_, s_b, gbc_b], [hab])
                                pt, pb = nextps()
                                ptb = pt[:].bitcast(BF16)
                                for c in range(8):
                                    kb.tr(ptb[:, c * 128:(c + 1) * 128], ha[:, c * 128:(c + 1) * 128], ident_b, [hab, cbf_b], [pb])
                                kb.copy("act", hT[:, :, i * 128:(i + 1) * 128],
                                        ptb[:, 0:1024].rearrange("p (c t) -> p c t", c=8), [pb], [hT_b])
                    S.barrier()

                    if 'B1' in PH:
                        with ExitStack() as st:
                            uT, uT_b = kb.sb(st, "uT", [128, L], BF16)
                            Bst, Bst_b = kb.sb(st, "BstL", [128, 32, 128], BF16)
                            Cst, Cst_b = kb.sb(st, "CstL", [128, 32, 128], BF16)
                            kb.load(Bst[:].rearrange("p g n -> p (g n)"), Bst_b, bst_d[l], bc_b[l])
                            kb.load(Cst[:].rearrange("p g n -> p (g n)"), Cst_b, cst_d[l], bc_b[l])
                            NLANE = int(os.environ.get('NLANE', '2'))
                            sbb = [[kb.sb(st, "sbb%d_%d" % (ln, i), [128, PAD + L], BF16) for i in range(2)] for ln in range(NLANE)]
                            qp = [[kb.sb(st, "qp%d_%d" % (ln, i), [128, 16, 128], BF16) for i in range(2)] for ln in range(NLANE)]
                            ytmp, ytmp_b = kb.sb(st, "ytmp", [128, 512], F32)
                            for ln in range(NLANE):
                                for i in range(2):
                                    kb.memset("pool", sbb[ln][i][0][:, 0:PAD], 0.0, [sbb[ln][i][1]])
                            evi = [0]

                            def evac(dst_ap, dstb, pt, pb):
                                evi[0] += 1
                                kb.copy("act" if evi[0] % 2 else "dve", dst_ap, pt[:], [pb], [dstb])
                            for c in range(4):
                                wv, wb_ = loadw("w_in", l, 8, c * 128, 128)
                                for b in range(4):
                                    pt, pb = nextps((0, 1, 2, 3))
                                    for kc in range(8):
                                        kb.mm(pt[:], wv[:, kc, :], hT[:, kc, b * 512:(b + 1) * 512], kc == 0, kc == 7,
                                              [wb_, hT_b], [pb])
                                    kb.copy("act", uT[:, b * 512:(b + 1) * 512], pt[:], [pb], [uT_b])
                                for g0 in range(0, 8, NLANE):
                                    lanes = []
                                    for ln in range(min(NLANE, 8 - g0)):
                                        gi = g0 + ln
                                        g = 8 * c + gi
                                        qa, qab = qp[ln][(g0 // NLANE) % 2]
                                        kb.load(qa[:], qab, qpow_d[l, g].rearrange("p (k n) -> p k n", k=16), qpow_b[l])
                                        lanes.append((gi, g, qa, qab, sbb[ln]))
                                    for gi, g, qa, qab, sb_ in lanes:
                                        for b in range(4):
                                            pt, pb = nextps((0, 1, 2, 3))
                                            kb.mm(pt[:], Bst[:, g, :], uT[:, b * 512:(b + 1) * 512], True, True, [Bst_b, uT_b], [pb])
                                            evac(sb_[0][0][:, PAD + b * 512:PAD + (b + 1) * 512], sb_[0][1], pt, pb)
                                    for k in range(6):
                                        base = 4 ** k
                                        for gi, g, qa, qab, sb_ in lanes:
                                            src, srcb = sb_[k % 2]
                                            dst, dstb = sb_[(k + 1) % 2]
                                            for b in range(4):
                                                mats = []
                                                for j in ((1, 2, 3) if k < 5 else (1,)):
                                                    m = j * base
                                                    if (b + 1) * 512 - m > 0:
                                                        mats.append((qa[:, 3 * k + j - 1, :], qab, PAD + b * 512 - m))
                                                dsl = slice(PAD + b * 512, PAD + (b + 1) * 512)
                                                if not mats:
                                                    kb.copy("pool", dst[:, dsl], src[:, dsl], [srcb], [dstb])
                                                    continue
                                                use_id = S5ID == 1 or (S5ID == 2 and b == 3) or (S5ID == 3 and b % 2 == 1)
                                                if use_id:
                                                    mats = [(ident_b, cbf_b, PAD + b * 512)] + mats
                                                pt, pb = nextps((0, 1, 2, 3))
                                                for mi, (lh, lhb, o0) in enumerate(mats):
                                                    kb.mm(pt[:], lh, src[:, o0:o0 + 512], mi == 0, mi == len(mats) - 1,
                                                          [lhb, srcb], [pb])
                                                if use_id:
                                                    kb.copy("act", dst[:, dsl], pt[:], [pb], [dstb])
                                                else:
                                                    kb.tt("dve", dst[:, dsl], pt[:], src[:, dsl], ALU.add, [pb, srcb], [dstb])
                                    for gi, g, qa, qab, sb_ in lanes:
                                        fin, finb = sb_[0]
                                        for b in range(4):
                                            yt, yb = psb[4 + b]
                                            kb.mm(yt[:], Cst[:, g, :], fin[:, PAD + b * 512:PAD + (b + 1) * 512], gi == 0, gi == 7,
                                                  [Cst_b, finb], [yb])
                                for b in range(4):
                                    yt, yb = psb[4 + b]
                                    kb.stt("dve", ytmp[:], uT[:, b * 512:(b + 1) * 512], dsk[:, c:c + 1], yt[:], ALU.mult, ALU.add,
                                           [uT_b, dsk_b, yb], [ytmp_b])
                                    kb.act(ysT[:, c, b * 512:(b + 1) * 512], ytmp[:], AF.Gelu_apprx_tanh, [ytmp_b], [ysT_b])
                    S.barrier()

                    if 'B2' in PH:
                        with ExitStack() as st:
                            qT = [kb.sb(st, "qT%d" % p, [128, L], BF16) for p in range(3)]
                            kAB = [kb.sb(st, "kAB%d" % i, [128, L], BF16) for i in range(2)]
                            Vp = [kb.sb(st, "Vp%d" % p, [128, 16, 128], BF16) for p in range(3)]
                            acc, acc_b = kb.sb(st, "acc", [128, 2, L], F32)
                            pT = [kb.sb(st, "pT%d" % i, [128, 256], BF16) for i in range(4)]
                            rden, rden_b = kb.sb(st, "rden", [128, L], F32)
                            vT, vT_b = kb.sb(st, "vT", [128, L], BF16)
                            kb.memset("pool", kAB[0][0][64:128, :], 0.0, [kAB[0][1]])
                            kb.memset("pool", kAB[1][0][0:64, :], 0.0, [kAB[1][1]])
                            pti = 0
                            for hp in range(4):
                                for p in range(3):
                                    wv, wb_ = loadw("w_in", l, 8, 512 + p * 512 + hp * 128, 128)
                                    for b in range(4):
                                        pt, pb = nextps()
                                        for kc in range(8):
                                            kb.mm(pt[:], wv[:, kc, :], hT[:, kc, b * 512:(b + 1) * 512], kc == 0, kc == 7,
                                                  [wb_, hT_b], [pb])
                                        kb.act(qT[p][0][:, b * 512:(b + 1) * 512], pt[:], AF.Copy, [pb], [qT[p][1]], scale=0.125)
                                wv, wb_ = loadw("w_in", l, 8, 2048 + hp * 128, 128)
                                for b in range(4):
                                    pt, pb = nextps()
                                    for kc in range(8):
                                        kb.mm(pt[:], wv[:, kc, :], hT[:, kc, b * 512:(b + 1) * 512], kc == 0, kc == 7,
                                              [wb_, hT_b], [pb])
                                    kb.copy("act", kAB[0][0][0:64, b * 512:(b + 1) * 512], pt[0:64, :], [pb], [kAB[0][1]])
                                    kb.copy("dve", kAB[1][0][64:128, b * 512:(b + 1) * 512], pt[64:128, :], [pb], [kAB[1][1]])
                                wv, wb_ = loadw("w_in", l, 8, 2560 + hp * 128, 128)
                                for b in range(4):
                                    pt, pb = nextps()
                                    for kc in range(8):
                                        kb.mm(pt[:], wv[:, kc, :], hT[:, kc, b * 512:(b + 1) * 512], kc == 0, kc == 7,
                                              [wb_, hT_b], [pb])
                                    kb.copy("dve", vT[:, b * 512:(b + 1) * 512], pt[:], [pb], [vT_b])
                                for p, (win, d) in enumerate(PATTERNS):
                                    nblk = L // d // 128
                                    blks = [(r, n) for r in range(d) for n in range(nblk)]
                                    for i0 in range(0, 16, 8):
                                        pt, pb = nextps()
                                        ptb = pt[:].bitcast(BF16)
                                        for j in range(8):
                                            r, n = blks[i0 + j]
                                            t0 = r + d * 128 * n
                                            kb.tr(ptb[:, j * 128:(j + 1) * 128], vT[:, t0:t0 + 127 * d + 1:d], ident_b, [vT_b, cbf_b], [pb])
                                        kb.copy("act", Vp[p][0][:, i0:i0 + 8, :],
                                                ptb[:, 0:1024].rearrange("p (j e) -> p j e", j=8), [pb], [Vp[p][1]])
                                pend = []

                                def flush(keep):
                                    while len(pend) > keep:
                                        pend.pop(0)()
                                for p, (win, d) in enumerate(PATTERNS):
                                    nblk = L // d // 128
                                    qt_, qtb = qT[p]
                                    vt_, vtb = Vp[p]
                                    for r in range(d):
                                        for n in range(nblk):
                                            t0 = r + d * 128 * n
                                            tq = slice(t0, t0 + 127 * d + 1, d)
                                            tprev = slice(t0 - 128 * d, t0 - d + 1, d)
                                            hasp = n >= 1
                                            wq = 256 if hasp else 128
                                            pnh = {}
                                            for hh in range(2):
                                                kt_, ktb = kAB[hh]
                                                ps_, psb_ = nextps()
                                                kb.mm(ps_[:, 0:128], kt_[:, tq], qt_[:, tq], True, False, [ktb, qtb], [psb_])
                                                kb.mm(ps_[:, 0:128], ident_b, mcur_b, False, True, [cbf_b], [psb_])
                                                if hasp:
                                                    kb.mm(ps_[:, 128:256], kt_[:, tprev], qt_[:, tq], True, False, [ktb, qtb], [psb_])
                                                    kb.mm(ps_[:, 128:256], ident_b, mprev_b, False, True, [cbf_b], [psb_])
                                                pe_, peb = pT[pti % 4]
                                                pti += 1
                                                kb.act(pe_[:, 0:wq], ps_[:, 0:wq], AF.Exp, [psb_], [peb])

                                                def pv(hh=hh, pe_=pe_, peb=peb, pnh=pnh, hasp=hasp, bid=r * nblk + n, tq=tq, p=p, vt_=vt_, vtb=vtb):
                                                    if hh == 0:
                                                        pnh["b"] = nextps()
                                                    pn, pnb = pnh["b"]
                                                    tp = (0, 64 * hh)
                                                    o_n = pn[64 * hh:64 * hh + 64, 0:128]
                                                    o_d = pn[64 * hh:64 * hh + 64, 128:256]
                                                    kb.mm(o_n, vt_[:, bid, 64 * hh:64 * hh + 64], pe_[:, 0:128], True, not hasp,
                                                          [vtb, peb], [pnb], tp=tp)
                                                    if hasp:
                                                        kb.mm(o_n, vt_[:, bid - 1, 64 * hh:64 * hh + 64], pe_[:, 128:256], False, True,
                                                              [vtb, peb], [pnb], tp=tp)
                                                    kb.mm(o_d, ones_b[:, 0:64], pe_[:, 0:128], True, not hasp, [cbf_b, peb], [pnb], tp=tp)
                                                    if hasp:
                                                        kb.mm(o_d, ones_b[:, 0:64], pe_[:, 128:256], False, True, [cbf_b, peb], [pnb], tp=tp)
                                                    if hh == 1:
                                                        src2 = pn[:, 0:256].rearrange("p (a t) -> p a t", a=2)
                                                        if p == 0:
                                                            kb.copy("dve", acc[:, :, tq], src2, [pnb], [acc_b])
                                                        else:
                                                            kb.tt("dve", acc[:, :, tq], acc[:, :, tq], src2, ALU.add, [acc_b, pnb], [acc_b])
                                                pend.append(pv)
                                                flush(2)
                                flush(0)
                                kb.recip(rden[:], acc[:, 1, :], [acc_b], [rden_b])
                                kb.tt("dve", oT[:, hp, :], acc[:, 0, :], rden[:], ALU.mult, [acc_b, rden_b], [oT_b])
                    S.barrier()

                    if 'B3' in PH:
                        with ExitStack() as st:
                            cwr, cwr_b = kb.sb(st, "cwr", [31, 512], F32)
                            kb.load(cwr[:], cwr_b, P["conv_w"][l])
                            cwT, cwT_b = kb.sb(st, "cwT", [128, 4, 31], F32)
                            cvp, cvp_b = kb.sb(st, "cvp", [128, 12], F32)
                            kb.load(cvp[:, 0:4], cvp_b, P["conv_b"][l].rearrange("(c p) -> p c", p=128), slow=True)
                            kb.load(cvp[:, 4:8], cvp_b, P["conv_ln_g"][l].rearrange("(c p) -> p c", p=128), slow=True)
                            kb.load(cvp[:, 8:12], cvp_b, P["conv_ln_b"][l].rearrange("(c p) -> p c", p=128), slow=True)
                            for c in range(4):
                                pt, pb = nextps()
                                kb.tr(pt[:, 0:31], cwr[:, c * 128:(c + 1) * 128], ident_f[0:31, 0:31], [cwr_b, cst_b], [pb])
                                kb.copy("dve", cwT[:, c, :], pt[:, 0:31], [pb], [cwT_b])
                            dg, dg_b = kb.sb(st, "dg", [128, 31, 128], BF16)
                            hpad, hpad_b = kb.sb(st, "hpad", [128, 30 + L], BF16)
                            sg_, sg_b = kb.sb(st, "sgl", [128, 512], F32)
                            cf, cf_b = kb.sb(st, "cf", [128, 4, L], F32)
                            sq2, sq2_b = kb.sb(st, "sq2", [128, 512], F32)
                            kb.memset("pool", hpad[:, 0:30], 0.0, [hpad_b])
                            for c in range(4):
                                for j in range(31):
                                    kb.ts("dve", dg[:, j, :], ident_f, cwT[:, c, j:j + 1], None, ALU.mult, None,
                                          [cst_b, cwT_b], [dg_b])
                                wa, wab = loadw("w_in", l, 8, 3072 + c * 128, 128)
                                wg, wgb = loadw("w_in", l, 8, 3584 + c * 128, 128)
                                for b in range(4):
                                    pa, pab = nextps()
                                    pg, pgb = nextps()
                                    for kc in range(8):
                                        kb.mm(pa[:], wa[:, kc, :], hT[:, kc, b * 512:(b + 1) * 512], kc == 0, kc == 7, [wab, hT_b], [pab])
                                    for kc in range(8):
                                        kb.mm(pg[:], wg[:, kc, :], hT[:, kc, b * 512:(b + 1) * 512], kc == 0, kc == 7, [wgb, hT_b], [pgb])
                                    kb.act(sg_[:], pg[:], AF.Sigmoid, [pgb], [sg_b])
                                    kb.tt("dve", hpad[:, 30 + b * 512:30 + (b + 1) * 512], pa[:], sg_[:], ALU.mult, [pab, sg_b], [hpad_b])
                                for b in range(4):
                                    pt, pb = nextps()
                                    for j in range(31):
                                        kb.mm(pt[:], dg[:, j, :], hpad[:, b * 512 + j:b * 512 + j + 512], j == 0, j == 30,
                                              [dg_b, hpad_b], [pb])
                                    kb.act(cf[:, c, b * 512:(b + 1) * 512], pt[:], AF.Identity, [pb, cvp_b], [cf_b], bias=cvp[:, c:c + 1])
                            mr, mr_b = kb.sb(st, "mr", [128, 2, 512], F32)
                            xc, xc_b = kb.sb(st, "xc", [128, 512], F32)
                            for b in range(4):
                                pm, pmb = nextps()
                                pv, pvb = nextps()
                                for c in range(4):
                                    kb.mm(pm[:], ones_f, cf[:, c, b * 512:(b + 1) * 512], c == 0, c == 3, [cst_b, cf_b], [pmb])
                                for c in range(4):
                                    kb.act(sq2[:], cf[:, c, b * 512:(b + 1) * 512], AF.Square, [cf_b], [sq2_b])
                                    kb.mm(pv[:], ones_f, sq2[:], c == 0, c == 3, [cst_b, sq2_b], [pvb])
                                kb.ts("dve", mr[:, 0, :], pm[:], 1.0 / 512, None, ALU.mult, None, [pmb], [mr_b])
                                kb.tt("dve", xc[:], mr[:, 0, :], mr[:, 0, :], ALU.mult, [mr_b], [xc_b])
                                kb.stt("dve", xc[:], pv[:], 1.0 / 512, xc[:], ALU.mult, ALU.subtract, [pvb, xc_b], [xc_b])
                                kb.ts("dve", xc[:], xc[:], 1e-6, None, ALU.add, None, [xc_b], [xc_b])
                                kb.act(xc[:], xc[:], AF.Sqrt, [xc_b], [xc_b])
                                kb.recip(mr[:, 1, :], xc[:], [xc_b], [mr_b])
                                for c in range(4):
                                    kb.tt("dve", xc[:], cf[:, c, b * 512:(b + 1) * 512], mr[:, 0, :], ALU.subtract, [cf_b, mr_b], [xc_b])
                                    kb.tt("dve", xc[:], xc[:], mr[:, 1, :], ALU.mult, [xc_b, mr_b], [xc_b])
                                    kb.act(hcT[:, c, b * 512:(b + 1) * 512], xc[:], AF.Silu, [xc_b, cvp_b], [hcT_b],
                                           bias=cvp[:, 8 + c:9 + c], scale=cvp[:, 4 + c:5 + c])
                    S.barrier()

                    if 'C' in PH:
                        with ExitStack() as st:
                            mg, mg_b = kb.sb(st, "mg", [128, 8, L], BF16)
                            bg, bg_b = kb.sb(st, "bg", [128, 24], F32)
                            kb.load(bg[:], bg_b, P["b_gate"][l].rearrange("(j p) -> p j", p=128), slow=True)
                            gs = [kb.sb(st, "gs%d" % i, [128, 512], F32) for i in range(3)]
                            ta, ta_b = kb.sb(st, "ta", [128, 512], F32)
                            tb2, tb2_b = kb.sb(st, "tb2", [128, 512], F32)
                            wcb = [kb.sb(st, "wcb%d" % i, [128, 1024], BF16) for i in range(7)]

                            def loadc(i, name, kc, c0):
                                t, b = wcb[i]
                                dst = t[:, 0:kc * 128].rearrange("p (k n) -> p k n", k=kc)
                                src = WB[name][l, :, c0:c0 + 128].rearrange("(k p) n -> p k n", p=128)
                                kb.load(dst, b, src, WBbuf[name][l])
                                return dst, b
                            for fc in range(8):
                                wg3 = [loadc(br, "w_in", 8, 4096 + br * 1024 + fc * 128) for br in range(3)]
                                wz1 = loadc(3, "w_ssm_glu", 4, fc * 128)
                                wz2 = loadc(4, "w_ssm_glu", 4, 1024 + fc * 128)
                                wau = loadc(5, "w_att_up", 4, fc * 128)
                                wpw = loadc(6, "w_conv_pw2", 4, fc * 128)
                                for b in range(4):
                                    bs = slice(b * 512, (b + 1) * 512)
                                    for br in range(3):
                                        pt, pb = nextps()
                                        for kc in range(8):
                                            kb.mm(pt[:], wg3[br][0][:, kc, :], hT[:, kc, bs], kc == 0, kc == 7, [wg3[br][1], hT_b], [pb])
                                        kb.act(gs[br][0][:], pt[:], AF.Sigmoid, [pb, bg_b], [gs[br][1]],
                                               bias=bg[:, br * 8 + fc:br * 8 + fc + 1])
                                    p1, p1b = nextps()
                                    p2, p2b = nextps()
                                    for kc in range(4):
                                        kb.mm(p1[:], wz1[0][:, kc, :], ysT[:, kc, bs], kc == 0, kc == 3, [wz1[1], ysT_b], [p1b])
                                    for kc in range(4):
                                        kb.mm(p2[:], wz2[0][:, kc, :], ysT[:, kc, bs], kc == 0, kc == 3, [wz2[1], ysT_b], [p2b])
                                    kb.act(ta[:], p2[:], AF.Sigmoid, [p2b], [ta_b])
                                    kb.tt("dve", ta[:], p1[:], ta[:], ALU.mult, [p1b, ta_b], [ta_b])
                                    kb.tt("dve", ta[:], ta[:], gs[0][0][:], ALU.mult, [ta_b, gs[0][1]], [ta_b])
                                    p3, p3b = nextps()
                                    for kc in range(4):
                                        kb.mm(p3[:], wau[0][:, kc, :], oT[:, kc, bs], kc == 0, kc == 3, [wau[1], oT_b], [p3b])
                                    kb.tt("dve", tb2[:], p3[:], gs[1][0][:], ALU.mult, [p3b, gs[1][1]], [tb2_b])
                                    kb.tt("pool", ta[:], ta[:], tb2[:], ALU.add, [ta_b, tb2_b], [ta_b])
                                    p4, p4b = nextps()
                                    for kc in range(4):
                                        kb.mm(p4[:], wpw[0][:, kc, :], hcT[:, kc, bs], kc == 0, kc == 3, [wpw[1], hcT_b], [p4b])
                                    kb.tt("dve", tb2[:], p4[:], gs[2][0][:], ALU.mult, [p4b, gs[2][1]], [tb2_b])
                                    kb.tt("dve", mg[:, fc, bs], ta[:], tb2[:], ALU.add, [ta_b, tb2_b], [mg_b])
                            wo = [loadw("w_out", l, 8, hf * 512, 512) for hf in range(2)]
                            xt = [kb.sb(st, "xc%d" % i, [128, D], F32) for i in range(2)]
                            for i in range(16):
                                xa, xab = xt[i % 2]
                                src, srcb = xsrc(i)
                                kb.load(xa[:], xab, src, srcb)
                                for hf in range(2):
                                    pt, pb = nextps()
                                    for kc in range(8):
                                        kb.mm(pt[:], mg[:, kc, i * 128:(i + 1) * 128], wo[hf][0][:, kc, :], kc == 0, kc == 7,
                                              [mg_b, wo[hf][1]], [pb])
                                    kb.tt("dve", xa[:, hf * 512:(hf + 1) * 512], xa[:, hf * 512:(hf + 1) * 512], pt[:], ALU.add,
                                          [xab, pb], [xab])
                                kb.store(xmid_d[i * 128:(i + 1) * 128, :], xmid_b[i], xa[:], xab)
                    S.barrier()

                if 'E' in PH:
                    with ExitStack() as st:
                        gbc, gbc_b = kb.sb(st, "g2bc", [128, D], F32)
                        kb.load(gbc[:], gbc_b, P["norm2_g"][l:l + 1, :].broadcast_to([128, D]))
                        if last:
                            gfc, gfc_b = kb.sb(st, "gfbc", [128, D], F32)
                            kb.load(gfc[:], gfc_b, final_g_d[0:1, :].broadcast_to([128, D]))
                        elif FUSEA:
                            gnx, gnx_b = kb.sb(st, "gnxbc", [128, D], F32)
                            kb.load(gnx[:], gnx_b, P["norm1_g"][l + 1:l + 2, :].broadcast_to([128, D]))
                            hbn = [kb.sb(st, "hbn%d" % i, [128, D], BF16) for i in range(2)]
                        xm2 = [[kb.sb(st, "xm%d_%d" % (q, i), [128, D], F32) for i in range(4)] for q in range(2)]
                        sqt, sqt_b = kb.sb(st, "sqtE", [128, D], F32)
                        hb = [kb.sb(st, "hbE%d" % i, [128, D], BF16) for i in range(4)]
                        sm = [kb.sb(st, "smE%d" % i, [128, 4], F32) for i in range(2)]
                        h2T2 = [kb.sb(st, "h2T%d" % q, [128, 8, 512], BF16) for q in range(2)]
                        gT, gT_b = kb.sb(st, "gT", [128, 22, 512], BF16)
                        wfo = [kb.sb(st, "wfo%d" % i, [128, 22, 512], BF16) for i in range(2)]
                        yo = [kb.sb(st, "yo%d" % i, [128, D], F32) for i in range(2)]
                        wfi = [0]

                        def norm_p1(b):
                            for j in range(4):
                                i = b * 4 + j
                                xa, xab = xm2[b % 2][j]
                                kb.load(xa[:], xab, xmid_d[i * 128:(i + 1) * 128, :], xmid_b[i])
                                kb.act(sqt[:], xa[:], AF.Square, [xab], [sqt_b])
                                s_, s_b = sm[i % 2]
                                kb.rsum(s_[:, 0:1], sqt[:], [sqt_b], [s_b])
                                kb.ts("dve", s_[:, 1:2], s_[:, 0:1], 1.0 / D, 1e-6, ALU.mult, ALU.add, [s_b], [s_b])
                                kb.act(s_[:, 2:3], s_[:, 1:2], AF.Sqrt, [s_b], [s_b])
                                kb.recip(s_[:, 3:4], s_[:, 2:3], [s_b], [s_b])
                                ha, hab = hb[j]
                                kb.stt("dve", ha[:], xa[:], s_[:, 3:4], gbc[:], ALU.mult, ALU.mult, [xab, s_b, gbc_b], [hab])

                        def norm_p2(b):
                            h2T, h2T_b = h2T2[b % 2]
                            for j in range(4):
                                ha, hab = hb[j]
                                pt, pb = nextps()
                                ptb = pt[:].bitcast(BF16)
                                for c in range(8):
                                    kb.tr(ptb[:, c * 128:(c + 1) * 128], ha[:, c * 128:(c + 1) * 128], ident_b, [hab, cbf_b], [pb])
                                kb.copy("act", h2T[:, :, j * 128:(j + 1) * 128],
                                        ptb[:, 0:1024].rearrange("p (c t) -> p c t", c=8), [pb], [h2T_b])

                        def ffn_in(b):
                            h2T, h2T_b = h2T2[b % 2]
                            for t in range(11):
                                wv, wb_ = loadw("w_ffn_in", l, 8, t * 512, 512)
                                for cq in range(4):
                                    cid = 4 * t + cq
                                    pt, pb = nextps()
                                    for kc in range(8):
                                        kb.mm(pt[:], wv[:, kc, cq * 128:(cq + 1) * 128], h2T[:, kc, :], kc == 0, kc == 7,
                                              [wb_, h2T_b], [pb])
                                    if cid < 22:
                                        kb.act(gT[:, cid, :], pt[:], AF.Silu, [pb], [gT_b])
                                    else:
                                        kb.tt("dve", gT[:, cid - 22, :], gT[:, cid - 22, :], pt[:], ALU.mult, [gT_b, pb], [gT_b])

                        def ffn_out_q(b, qd):
                            if qd % 2:
                                return
                            hf = qd // 2
                            wt, wtb = wfo[wfi[0] % 2]
                            wfi[0] += 1
                            kb.load(wt[:], wtb, WB["w_ffn_out"][l, :, hf * 512:(hf + 1) * 512].rearrange("(k p) n -> p k n", p=128),
                                    WBbuf["w_ffn_out"][l])
                            for j in range(4):
                                xa, xab = xm2[b % 2][j]
                                pt, pb = nextps()
                                for kc in range(22):
                                    kb.mm(pt[:], gT[:, kc, j * 128:(j + 1) * 128], wt[:, kc, :], kc == 0, kc == 21,
                                          [gT_b, wtb], [pb])
                                kb.tt("dve", xa[:, hf * 512:(hf + 1) * 512], xa[:, hf * 512:(hf + 1) * 512], pt[:], ALU.add,
                                      [xab, pb], [xab])
                        norm_p1(0)
                        norm_p2(0)
                        for b in range(4):
                            xm = xm2[b % 2]
                            ffn_in(b)
                            if b + 1 < 4:
                                norm_p1(b + 1)
                            ffn_out_q(b, 0)
                            ffn_out_q(b, 1)
                            if b + 1 < 4:
                                norm_p2(b + 1)
                            ffn_out_q(b, 2)
                            ffn_out_q(b, 3)
                            for j in range(4):
                                i = b * 4 + j
                                xa, xab = xm[j]
                                if not last:
                                    kb.store(xout_d[i * 128:(i + 1) * 128, :], xout_b[i], xa[:], xab)
                                    if FUSEA:
                                        kb.act(sqt[:], xa[:], AF.Square, [xab], [sqt_b])
                                        s_, s_b = sm[i % 2]
                                        kb.rsum(s_[:, 0:1], sqt[:], [sqt_b], [s_b])
                                        kb.ts("dve", s_[:, 1:2], s_[:, 0:1], 1.0 / D, 1e-6, ALU.mult, ALU.add, [s_b], [s_b])
                                        kb.act(s_[:, 2:3], s_[:, 1:2], AF.Sqrt, [s_b], [s_b])
                                        kb.recip(s_[:, 3:4], s_[:, 2:3], [s_b], [s_b])
                                        ha, hab = hbn[i % 2]
                                        kb.stt("dve", ha[:], xa[:], s_[:, 3:4], gnx[:], ALU.mult, ALU.mult, [xab, s_b, gnx_b], [hab])
                                        pt, pb = nextps()
                                        ptb = pt[:].bitcast(BF16)
                                        for c in range(8):
                                            kb.tr(ptb[:, c * 128:(c + 1) * 128], ha[:, c * 128:(c + 1) * 128], ident_b, [hab, cbf_b], [pb])
                                        kb.copy("act", hT[:, :, i * 128:(i + 1) * 128],
                                                ptb[:, 0:1024].rearrange("p (c t) -> p c t", c=8), [pb], [hT_b])
                                else:
                                    kb.act(sqt[:], xa[:], AF.Square, [xab], [sqt_b])
                                    s_, s_b = sm[i % 2]
                                    kb.rsum(s_[:, 0:1], sqt[:], [sqt_b], [s_b])
                                    kb.ts("dve", s_[:, 1:2], s_[:, 0:1], 1.0 / D, 1e-6, ALU.mult, ALU.add, [s_b], [s_b])
                                    kb.act(s_[:, 2:3], s_[:, 1:2], AF.Sqrt, [s_b], [s_b])
                                    kb.recip(s_[:, 3:4], s_[:, 2:3], [s_b], [s_b])
                                    ya, yab = yo[i % 2]
                                    kb.stt("dve", ya[:], xa[:], s_[:, 3:4], gfc[:], ALU.mult, ALU.mult, [xab, s_b, gfc_b], [yab])
                                    kb.store(out_d[sq * L + i * 128: sq * L + (i + 1) * 128, :], out_b, ya[:], yab)
                S.barrier()
        S.barrier(full=True)
        S.run(st0)
    return nc


def make_consts():
    c = np.zeros((128, NCONST), np.float32)
    i = np.arange(128)
    c[:, 0:128] = np.eye(128)
    j = np.zeros((128, 128), np.float32)
    j[i[:64], i[:64] + 64] = 1.0
    j[i[:64] + 64, i[:64]] = 1.0
    c[:, 128:256] = j
    k = i[:, None]
    q = i[None, :]
    c[:, 256:384] = np.where(k <= q, 0.0, NEG)
    c[:, 384:512] = np.where(k >= q, 0.0, NEG)
    c[:, 512:640] = 1.0
    c[:, 640:648] = (i[:, None] // 16 == np.arange(8)[None, :]).astype(np.float32)
    c[:64, 648] = -1.0
    c[64:, 648] = 1.0
    return c


_NC_CACHE = {}


def run(inputs, NSEQ, DEPTH, ncores=8):
    key = (NSEQ, DEPTH)
    if key not in _NC_CACHE:
        _NC_CACHE[key] = build(NSEQ, DEPTH)
    nc = _NC_CACHE[key]
    consts = make_consts()
    x = np.asarray(inputs["x"], np.float32)
    in_maps = []
    for c in range(ncores):
        m = {"x": np.ascontiguousarray(x[c * NSEQ:(c + 1) * NSEQ].reshape(NSEQ * L, D)),
             "consts": consts,
             "final_g": np.ascontiguousarray(np.asarray(inputs["final_g"], np.float32).reshape(1, D))}
        for n, shp in PNAMES:
            m[n] = np.ascontiguousarray(np.asarray(inputs[n], np.float32)[:DEPTH])
        in_maps.append(m)
    res = run_bass_kernel_spmd(nc, in_maps, core_ids=list(range(ncores)))
    outs = [r["out"].reshape(NSEQ, L, D) for r in res.results]
    return np.concatenate(outs, axis=0).astype(np.float32)


def kernel(**inputs):
    return run(inputs, 4, 2, 8)
```
